# Optimizing a Trainium2 kernel written in Bass

```python
import jax, jax.numpy as jnp
from jax import lax
import numpy as np


D_MODEL = 2048
BATCH = 32
SEQ = 256
DEPTH = 2
DEC_BATCH = 8
DEC_SEQ = 2048
PAST_LEN = 256

GRID_W = 64
BRANCH = D_MODEL // 2
N_SLICES = 6
N_DIR = 2
N_EVEN = (DEPTH + 1) // 2
N_ODD = DEPTH // 2
LRU_HEADS = 8
LRU_HEAD_DIM = BRANCH // LRU_HEADS
LRU_CONV = 4
LRU_C = 8.0
RET_HEADS = 8
RET_HEAD_DIM = BRANCH // RET_HEADS
RET_CHUNK = 128
FOURIER_GROUPS = 4
SCONV = 3
ROPE_BASE = 10000.0
EPS = 1e-6

kernel_name = 'hybrid_lru_retention_fnet_shortconv_prefix_dit'


def rmsnorm(x, g):
    xf = x.astype(jnp.float32)
    y = xf * lax.rsqrt(jnp.mean(xf * xf, axis=-1, keepdims=True) + EPS)
    return (y * g.astype(jnp.float32)).astype(x.dtype)


def modulation(cond, ada_w, ada_b):
    m = jax.nn.silu(cond) @ ada_w + ada_b
    shift, scale, gate = jnp.split(m[:, None, :], 3, axis=-1)
    return shift, scale, gate


def dwconv(x, w, pad_lo, pad_hi):
    return lax.conv_general_dilated(
        x, w.astype(x.dtype)[:, None, :], window_strides=(1,), padding=[(pad_lo, pad_hi)],
        dimension_numbers=('NWC', 'WIO', 'NWC'), feature_group_count=x.shape[-1])


def grid_rope(n_tokens):
    rows = n_tokens // GRID_W
    row = jnp.repeat(jnp.arange(rows, dtype=jnp.float32), GRID_W)
    col = jnp.tile(jnp.arange(GRID_W, dtype=jnp.float32), rows)
    n_freq = RET_HEAD_DIM // 4
    freqs = ROPE_BASE ** (-jnp.arange(n_freq, dtype=jnp.float32) / n_freq)
    ang = jnp.concatenate([row[:, None] * freqs, col[:, None] * freqs], axis=-1)
    return jnp.cos(ang), jnp.sin(ang)


def apply_rope(x, cos, sin):
    x1, x2 = jnp.split(x, 2, axis=-1)
    cs, sn = cos[:, None, :], sin[:, None, :]
    return jnp.concatenate([x1 * cs - x2 * sn, x1 * sn + x2 * cs], axis=-1)


def linear_scan(a, b, h0):
    def comb(l, r):
        return l[0] * r[0], r[0] * l[1] + r[1]
    a_cum, h = lax.associative_scan(comb, (a, b), axis=1)
    return h + a_cum * h0[:, None, :]


def rglru_dir(u, h0, lam, w_r, b_r, w_i, b_i):
    bsz, t, w = u.shape
    uh = u.reshape(bsz, t, LRU_HEADS, LRU_HEAD_DIM)
    r = jax.nn.sigmoid(jnp.einsum('bthi,hij->bthj', uh, w_r).reshape(bsz, t, w) + b_r)
    i = jax.nn.sigmoid(jnp.einsum('bthi,hij->bthj', uh, w_i).reshape(bsz, t, w) + b_i)
    log_a = -LRU_C * r * jax.nn.softplus(-lam.astype(jnp.float32))
    a = jnp.exp(log_a)
    norm_in = jnp.sqrt(-jnp.expm1(2.0 * log_a))
    h = linear_scan(a, norm_in * (i * u), h0)
    return h, h[:, -1]


def lru_mixer(xa, h0s, conv_w, conv_b, lam, w_r, b_r, w_i, b_i):
    u = (dwconv(xa, conv_w, 2, 1) + conv_b).astype(jnp.float32)
    h0s = h0s.astype(jnp.float32)
    hf, sf = rglru_dir(u, h0s[:, 0], lam[0], w_r[0], b_r[0], w_i[0], b_i[0])
    hb, sb = rglru_dir(jnp.flip(u, 1), h0s[:, 1], lam[1], w_r[1], b_r[1], w_i[1], b_i[1])
    return hf + jnp.flip(hb, 1), jnp.stack([sf, sb], axis=1)


def retention_dir(q, k, v, s0, log_g):
    bsz, nh, t, d = q.shape
    n_chunks = t // RET_CHUNK
    idx = jnp.arange(RET_CHUNK, dtype=jnp.float32)
    diff = idx[:, None] - idx[None, :]
    mask = jnp.where(diff >= 0, jnp.exp(log_g[:, None, None] * jnp.maximum(diff, 0.0)), 0.0)
    q_dec = jnp.exp(log_g[:, None] * (idx + 1.0))[..., None]
    k_dec = jnp.exp(log_g[:, None] * (RET_CHUNK - 1.0 - idx))[..., None]
    chunk_dec = jnp.exp(log_g * RET_CHUNK)[:, None, None]

    def to_chunks(z):
        return z.reshape(bsz, nh, n_chunks, RET_CHUNK, d).transpose(2, 0, 1, 3, 4)

    def step(s, qkv):
        qi, ki, vi = qkv
        scores = jnp.einsum('bhid,bhjd->bhij', qi, ki) * mask
        o = jnp.einsum('bhij,bhje->bhie', scores, vi) + jnp.einsum('bhid,bhde->bhie', qi * q_dec, s)
        s = s * chunk_dec + jnp.einsum('bhjd,bhje->bhde', ki * k_dec, vi)
        return s, o

    s_fin, oc = lax.scan(step, s0, (to_chunks(q), to_chunks(k), to_chunks(v)))
    o = oc.transpose(1, 2, 0, 3, 4).reshape(bsz, nh, t, d)
    return o, s_fin


def retention_mixer(q, k, v, s0s, ret_decay, ret_norm_g, rope):
    bsz, t, w = q.shape
    shp = (bsz, t, RET_HEADS, RET_HEAD_DIM)
    qh = q.astype(jnp.float32).reshape(shp)
    kh = k.astype(jnp.float32).reshape(shp) * (RET_HEAD_DIM ** -0.5)
    vh = v.astype(jnp.float32).reshape(shp)
    if rope is not None:
        qh = apply_rope(qh, *rope)
        kh = apply_rope(kh, *rope)
    qh, kh, vh = (z.transpose(0, 2, 1, 3) for z in (qh, kh, vh))
    log_g = jax.nn.log_sigmoid(ret_decay.astype(jnp.float32))
    s0s = s0s.astype(jnp.float32)
    of, sf = retention_dir(qh, kh, vh, s0s[:, 0], log_g[0])
    ob, sb = retention_dir(jnp.flip(qh, 2), jnp.flip(kh, 2), jnp.flip(vh, 2), s0s[:, 1], log_g[1])
    o = (of + jnp.flip(ob, 2)).transpose(0, 2, 1, 3)
    o = o * lax.rsqrt(jnp.mean(o * o, axis=-1, keepdims=True) + EPS)
    o = o.reshape(bsz, t, w) * ret_norm_g.astype(jnp.float32)
    return o, jnp.stack([sf, sb], axis=1)


def fourier_mixer(u):
    bsz, t, w = u.shape
    ug = u.astype(jnp.float32).reshape(bsz, t, FOURIER_GROUPS, w // FOURIER_GROUPS)
    y = jnp.fft.fftn(ug, axes=(1, 3), norm='ortho').real
    return y.reshape(bsz, t, w)


def shortconv_mixer(gb, gc, xd, conv_w):
    return (gb * dwconv(gc * xd, conv_w, 1, 1)).astype(jnp.float32)


def layer_lru_retention(x, cond, h0_lru, s0_ret, rope, norm_g, ada_w, ada_b, w_in, w_out,
                        conv_w, conv_b, lam, w_r, b_r, w_i, b_i, ret_decay, ret_norm_g):
    shift, scale, gate = modulation(cond, ada_w, ada_b)
    h = rmsnorm(x, norm_g) * (1.0 + scale) + shift
    xa, ga, q, k, v, gb = jnp.split(h @ w_in, N_SLICES, axis=-1)
    ya, s_lru = lru_mixer(xa, h0_lru, conv_w, conv_b, lam, w_r, b_r, w_i, b_i)
    yb, s_ret = retention_mixer(q, k, v, s0_ret, ret_decay, ret_norm_g, rope)
    y = jnp.concatenate([ya * jax.nn.silu(ga.astype(jnp.float32)),
                         yb * jax.nn.silu(gb.astype(jnp.float32))], axis=-1).astype(x.dtype)
    return x + gate * (y @ w_out), s_lru, s_ret


def layer_fourier_shortconv(x, cond, norm_g, ada_w, ada_b, w_in, w_out, conv_w):
    shift, scale, gate = modulation(cond, ada_w, ada_b)
    h = rmsnorm(x, norm_g) * (1.0 + scale) + shift
    xc, gc, bd, cd, xd, gd = jnp.split(h @ w_in, N_SLICES, axis=-1)
    yc = fourier_mixer(xc)
    yd = shortconv_mixer(bd, cd, xd, conv_w)
    y = jnp.concatenate([yc * jax.nn.silu(gc.astype(jnp.float32)),
                         yd * jax.nn.silu(gd.astype(jnp.float32))], axis=-1).astype(x.dtype)
    return x + gate * (y @ w_out)


def setup_inputs(seed: int = 0) -> dict:
    key = jax.random.key(seed)
    ks = iter(jax.random.split(key, 32))
    f32 = jnp.float32

    def nrm(shape, s):
        return s * jax.random.normal(next(ks), shape, f32)

    W, D = BRANCH, D_MODEL
    x_prompt = nrm((BATCH, SEQ, D), 1.0)
    x_sample = nrm((DEC_BATCH, DEC_SEQ, D), 1.0)
    state_lru = nrm((DEC_BATCH, N_EVEN, N_DIR, W), 0.5)
    state_ret = nrm((DEC_BATCH, N_EVEN, N_DIR, RET_HEADS, RET_HEAD_DIM, RET_HEAD_DIM), 1.0)
    c = nrm((DEC_BATCH, D), 1.0)
    c_ctx = nrm((D,), 1.0)
    norm_g = 1.0 + nrm((DEPTH, D), 0.02)
    ada_w = nrm((DEPTH, D, 3 * D), 0.5 * D ** -0.5)
    ada_b = nrm((DEPTH, 3 * D), 0.01)
    w_in = nrm((DEPTH, D, N_SLICES * W), D ** -0.5)
    w_out = nrm((DEPTH, 2 * W, D), (2 * W) ** -0.5)
    lru_conv_w = nrm((N_EVEN, LRU_CONV, W), LRU_CONV ** -0.5)
    lru_conv_b = nrm((N_EVEN, W), 0.01)
    a_c = jax.random.uniform(next(ks), (N_EVEN, N_DIR, W), f32, 0.9, 0.999)
    a = a_c ** (1.0 / LRU_C)
    lru_lambda = jnp.log(a) - jnp.log1p(-a)
    lru_w_r = nrm((N_EVEN, N_DIR, LRU_HEADS, LRU_HEAD_DIM, LRU_HEAD_DIM), LRU_HEAD_DIM ** -0.5)
    lru_b_r = nrm((N_EVEN, N_DIR, W), 0.01)
    lru_w_i = nrm((N_EVEN, N_DIR, LRU_HEADS, LRU_HEAD_DIM, LRU_HEAD_DIM), LRU_HEAD_DIM ** -0.5)
    lru_b_i = nrm((N_EVEN, N_DIR, W), 0.01)
    expo = 5.0 + jnp.arange(RET_HEADS, dtype=f32)
    ret_decay = jnp.log(2.0 ** expo - 1.0) + nrm((N_EVEN, N_DIR, RET_HEADS), 0.1)
    ret_norm_g = 1.0 + nrm((N_EVEN, W), 0.02)
    sconv_w = nrm((N_ODD, SCONV, W), SCONV ** -0.5)
    final_norm_g = 1.0 + nrm((D,), 0.02)
    return {'x_prompt': x_prompt, 'x_sample': x_sample, 'state_lru': state_lru, 'state_ret': state_ret,
            'c': c, 'c_ctx': c_ctx, 'norm_g': norm_g, 'ada_w': ada_w, 'ada_b': ada_b,
            'w_in': w_in, 'w_out': w_out, 'lru_conv_w': lru_conv_w, 'lru_conv_b': lru_conv_b,
            'lru_lambda': lru_lambda, 'lru_w_r': lru_w_r, 'lru_b_r': lru_b_r, 'lru_w_i': lru_w_i,
            'lru_b_i': lru_b_i, 'ret_decay': ret_decay, 'ret_norm_g': ret_norm_g,
            'sconv_w': sconv_w, 'final_norm_g': final_norm_g}


def reference(x_prompt, x_sample, state_lru, state_ret, c, c_ctx, norm_g, ada_w, ada_b, w_in, w_out,
              lru_conv_w, lru_conv_b, lru_lambda, lru_w_r, lru_b_r, lru_w_i, lru_b_i,
              ret_decay, ret_norm_g, sconv_w, final_norm_g):
    rope = grid_rope(x_sample.shape[1])
    cond_ctx = c_ctx[None, :]
    b_ctx = x_prompt.shape[0]
    zero_lru = jnp.zeros((b_ctx, N_DIR, BRANCH), jnp.float32)
    zero_ret = jnp.zeros((b_ctx, N_DIR, RET_HEADS, RET_HEAD_DIM, RET_HEAD_DIM), jnp.float32)
    xp, xs = x_prompt, x_sample
    lru_states, ret_states = [], []
    for l in range(DEPTH):
        j = l // 2
        shared = (norm_g[l], ada_w[l], ada_b[l], w_in[l], w_out[l])
        if l % 2 == 0:
            mix = (lru_conv_w[j], lru_conv_b[j], lru_lambda[j], lru_w_r[j], lru_b_r[j],
                   lru_w_i[j], lru_b_i[j], ret_decay[j], ret_norm_g[j])
            xp, s_lru, s_ret = layer_lru_retention(xp, cond_ctx, zero_lru, zero_ret, None, *shared, *mix)
            xs, _, _ = layer_lru_retention(xs, c, state_lru[:, j], state_ret[:, j], rope, *shared, *mix)
            lru_states.append(s_lru)
            ret_states.append(s_ret)
        else:
            xp = layer_fourier_shortconv(xp, cond_ctx, *shared, sconv_w[j])
            xs = layer_fourier_shortconv(xs, c, *shared, sconv_w[j])
    y_prompt = rmsnorm(xp, final_norm_g)
    y_sample = rmsnorm(xs, final_norm_g)
    new_state_lru = jnp.stack(lru_states, axis=1).astype(x_prompt.dtype)
    new_state_ret = jnp.stack(ret_states, axis=1).astype(x_prompt.dtype)
    return (y_prompt, y_sample, new_state_lru, new_state_ret)
```

```python
import numpy as np
import ml_dtypes
import concourse.bass as bass
import concourse.mybir as mybir
from concourse.bass_utils import run_bass_kernel_spmd

F32 = mybir.dt.float32
BF16 = mybir.dt.bfloat16
AF = mybir.ActivationFunctionType
ALU = mybir.AluOpType

D = 2048
W = 1024
NCORES = 8
TS, TP = 2048, 1024
TT = TS + TP
EPS = 1e-6
NBIG = 11
PA_STEP = 99
EVAC_ACT_ONLY = True
PA_TILES = 999
N_LRU = 8
LRU_STEP = 99
NROT = 5
BG_PER_PROJ = 1
PA_NBUF = 2
EVAC_SKIP = ()
EVAC_DST0 = False
PA_SRCMOD = 0
N_RET = 8


class Res:
    __slots__ = ("w", "r", "name", "parent", "children")

    def __init__(self, name="", parent=None):
        self.w = {}
        self.r = {}
        self.name = name
        self.parent = parent
        self.children = []
        if parent is not None:
            parent.children.append(self)


class Sem:
    __slots__ = ("h", "n")

    def __init__(self, h):
        self.h = h
        self.n = 0


class Eng:
    def __init__(self, name, handle, sem):
        self.name = name
        self.h = handle
        self.sem = sem
        self.seen = {}


class TK:
    def __init__(self, nc):
        self.nc = nc
        self.sems = []
        self.eng = {}
        for name, h in (("pe", nc.tensor), ("act", nc.scalar), ("dve", nc.vector),
                        ("pool", nc.gpsimd), ("sp", nc.sync)):
            self.eng[name] = Eng(name, h, self.newsem(name))

    def newsem(self, name):
        s = Sem(self.nc.alloc_semaphore("s_" + name + str(len(self.sems))))
        self.sems.append(s)
        return s

    def _collect(self, reads, writes, pw):
        deps = {}

        def add(d):
            for s, v in d.items():
                if deps.get(s, 0) < v:
                    deps[s] = v
        def addw(w):
            add(w.w)
            add(w.r)
            if w.parent is not None:
                add(w.parent.w)
                add(w.parent.r)
            for c in w.children:
                add(c.w)
                add(c.r)
        for r in reads:
            add(r.w)
            if r.parent is not None:
                add(r.parent.w)
            for c in r.children:
                add(c.w)
        for w in writes:
            addw(w)
        for w in pw:
            if w.r or any(c.r or c.w for c in w.children):
                addw(w)
        return deps

    def _wait(self, e, deps, skip_self):
        for s, v in deps.items():
            if skip_self and s is e.sem:
                continue
            if e.seen.get(s, 0) >= v:
                continue
            e.h.wait_ge(s.h, v)
            e.seen[s] = v

    def _commit(self, ev_s, ev_v, reads, writes, pw):
        for r in reads:
            if r.r.get(ev_s, 0) < ev_v:
                r.r[ev_s] = ev_v
        for w in writes:
            w.w = {ev_s: ev_v}
            w.r = {}
            for c in w.children:
                c.w = {}
                c.r = {}
        for w in pw:
            if w.r or any(c.r or c.w for c in w.children):
                w.w = {ev_s: ev_v}
                w.r = {}
                for c in w.children:
                    c.w = {}
                    c.r = {}
            else:
                w.w[ev_s] = ev_v

    def op(self, eng, fn, reads=(), writes=(), pw=()):
        e = self.eng[eng]
        self._wait(e, self._collect(reads, writes, pw), skip_self=(eng == "pe"))
        ins = fn(e.h)
        e.sem.n += 1
        ins.then_inc(e.sem.h, 1)
        self._commit(e.sem, e.sem.n, reads, writes, pw)

    def group(self, eng, fns, reads=(), writes=(), pw=()):
        e = self.eng[eng]
        self._wait(e, self._collect(reads, writes, pw), skip_self=(eng == "pe"))
        ins = None
        for fn in fns:
            ins = fn(e.h)
        e.sem.n += 1
        ins.then_inc(e.sem.h, 1)
        self._commit(e.sem, e.sem.n, reads, writes, pw)

    def dma(self, q, sem, out, in_, reads=(), writes=(), pw=()):
        e = self.eng[q]
        self._wait(e, self._collect(reads, writes, pw), skip_self=False)
        ins = e.h.dma_start(out=out, in_=in_)
        sem.n += 16
        ins.then_inc(sem.h, 16)
        self._commit(sem, sem.n, reads, writes, pw)

    def final_wait(self, q, resources):
        e = self.eng[q]
        deps = self._collect((), resources, ())
        self._wait(e, deps, skip_self=False)


def _host_consts():
    c = {}
    f32 = np.float32
    idx = np.arange(128)
    ident = np.eye(128, dtype=f32)
    perm = np.zeros((128, 128), f32)
    perm[(idx + 64) % 128, idx] = 1.0
    maskf = (idx[None, :] >= idx[:, None]).astype(f32)
    maskb = (idx[:, None] >= idx[None, :]).astype(f32)
    eqf = np.broadcast_to((idx + 1.0)[None, :], (128, 128)).astype(f32)
    eqb = np.broadcast_to((128.0 - idx)[None, :], (128, 128)).astype(f32)
    onesm = np.full((128, 128), 1.0 / 128.0, f32)
    cols = np.zeros((128, 8), f32)
    cols[:, 0] = 127.0 - idx
    cols[:, 1] = idx
    cols[:, 2] = 128.0
    cols[:, 3] = 1.0
    cols[:, 4] = EPS
    c["cst"] = np.concatenate([ident, perm, maskf, maskb, eqf, eqb, onesm, cols], axis=1)
    c["identb"] = ident.astype(ml_dtypes.bfloat16)
    t = np.arange(2048)
    row = (t // 64).astype(np.float64)
    col = (t % 64).astype(np.float64)
    freqs = 10000.0 ** (-np.arange(32, dtype=np.float64) / 32)
    ang = np.concatenate([row[:, None] * freqs, col[:, None] * freqs], axis=-1)
    cs, sn = np.cos(ang).T, np.sin(ang).T
    c["ropeC"] = np.concatenate([cs, cs], 0).astype(f32)
    c["ropeS"] = np.concatenate([-sn, sn], 0).astype(f32)

    def dft(n):
        k = np.arange(n, dtype=np.float64)
        a = 2.0 * np.pi * np.outer(k, k) / n
        return np.cos(a) / np.sqrt(n), np.sin(a) / np.sqrt(n)
    ct, st = dft(2048)
    c["dftC"] = ct.astype(ml_dtypes.bfloat16)
    c["dftS"] = (-st).astype(ml_dtypes.bfloat16)
    ct, st = dft(256)
    c["dftC256"] = ct.astype(ml_dtypes.bfloat16)
    c["dftS256"] = (-st).astype(ml_dtypes.bfloat16)
    c["dftCC"] = np.concatenate([ct, st], axis=1).astype(ml_dtypes.bfloat16)
    return c


PF = {}


def _pack_params(inp):
    cols = []
    off = [0]

    def add(name, vec):
        v = np.asarray(vec, np.float32).reshape(-1, 128).T
        PF[name] = (off[0], v.shape[1])
        off[0] += v.shape[1]
        cols.append(v)
    for l in range(2):
        add(f"ng{l}", inp["norm_g"][l])
        add(f"absh{l}", inp["ada_b"][l][0:D])
        add(f"absc{l}", inp["ada_b"][l][D:2 * D])
        add(f"abg{l}", inp["ada_b"][l][2 * D:3 * D])
    for j in range(4):
        add(f"cw{j}", inp["lru_conv_w"][0][j])
    add("cb", inp["lru_conv_b"][0])
    for d in range(2):
        add(f"lam{d}", inp["lru_lambda"][0][d])
        add(f"br{d}", inp["lru_b_r"][0][d])
        add(f"bi{d}", inp["lru_b_i"][0][d])
    add("rng", inp["ret_norm_g"][0])
    for j in range(3):
        add(f"sw{j}", inp["sconv_w"][0][j])
    rd = np.broadcast_to(np.asarray(inp["ret_decay"][0], np.float32).reshape(1, 16), (128, 16))
    PF["rdec"] = (off[0], 16)
    off[0] += 16
    cols.append(rd)
    return np.ascontiguousarray(np.concatenate(cols, axis=1))


class Group:
    def __init__(self, name, t0, T, L, nseq, row, rope):
        self.name, self.t0, self.T, self.L, self.nseq, self.row, self.rope = name, t0, T, L, nseq, row, rope


GS = Group("s", 0, TS, 2048, 1, 0, True)
GP = Group("p", TS, TP, 256, 4, 1, False)


def build(npf, debug=False, stop_after=None):
    nc = bass.Bass("TRN2", target_bir_lowering=False)
    tk = TK(nc)
    op, grp, dma = tk.op, tk.group, tk.dma

    def din(name, shape, dt=F32):
        return nc.dram_tensor(name, list(shape), dt, kind="ExternalInput").ap()

    def dout(name, shape, dt=F32):
        return nc.dram_tensor(name, list(shape), dt, kind="ExternalOutput").ap()

    def dscr(name, shape, dt=F32):
        return nc.dram_tensor(name, list(shape), dt, kind=("ExternalOutput" if debug else "Internal")).ap()

    x_in = din("x", [TT, D])
    pcore = din("pcore", [128, 48])
    st_ret = din("st_ret", [2, 8, 128, 128])
    pfm_d = din("pfm", [128, npf])
    cst_d = din("cst", [128, 7 * 128 + 8])
    identb_d = din("identb", [128, 128], BF16)
    ropeC_d = din("ropeC", [128, 2048])
    ropeS_d = din("ropeS", [128, 2048])
    dftC_d = din("dftC", [2048, 2048], BF16)
    dftS_d = din("dftS", [2048, 2048], BF16)
    dftC256_d = din("dftC256", [256, 256], BF16)
    dftS256_d = din("dftS256", [256, 256], BF16)
    dftCC_d = din("dftCC", [256, 512], BF16)
    ada_w = din("ada_w", [2, D, 3 * D])
    ada_b = din("ada_b", [2, 3 * D])
    w_in = din("w_in", [2, D, 6 * W])
    w_out = din("w_out", [2, D, D])
    lru_wr = din("lru_w_r", [2, 8, 128, 128])
    lru_wi = din("lru_w_i", [2, 8, 128, 128])
    fng_d = din("final_norm_g", [1, D])

    y_out = dout("y", [TT, D])
    olru = dout("o_lru", [64, 128])
    oret = dout("o_ret", [4, 2, 8, 128, 128])
    x1_d = dscr("x1", [TT, D])
    x2_d = dscr("x2", [TT, D])
    yT_d = dscr("yT", [16, 128, TT], BF16)

    def sb(name, shape, dt=F32):
        return nc.alloc_sbuf_tensor(name, list(shape), dt)

    H = sb("H", [128, 16, TS], BF16)
    wsl = [sb(f"wsl{i}", [128, 16, 128], BF16) for i in range(4)]
    big = [sb(f"big{i}", [128, 2048], F32) for i in range(NBIG)]
    flex = sb("flex", [128, 4096], F32)
    pfm = sb("pfm_sb", [128, npf])
    cst = sb("cst_sb", [128, 7 * 128 + 8])
    identb = sb("identb_sb", [128, 128], BF16)
    pc = sb("pc_sb", [128, 48])
    scond = sb("scond", [128, 16, 2], BF16)
    modAB = sb("modAB", [128, 2, 2, 3, 16])
    smallf = sb("smallf", [128, 64])
    lgt = sb("lgt", [128, 64])
    stl = sb("stl", [128, 64])
    onesb = sb("onesb", [128, 128], BF16)
    u2t = sb("u2t", [128, 2048])
    hb = sb("hb", [128, 48])
    rtab = sb("rtab", [128, 4, 128])
    wrb = big[9][:].bitcast(BF16)[:, 0:2048].rearrange("p (a b) -> p a b", b=128)
    wib = big[9][:].bitcast(BF16)[:, 2048:4096].rearrange("p (a b) -> p a b", b=128)
    sbf = big[4][:].bitcast(BF16).rearrange("p (d c e) -> p d c e", d=2, c=16)
    dft256 = sb("dft256", [128, 2, 2, 256], BF16)
    dftcc = sb("dftcc", [128, 2, 512], BF16)

    ps = [nc.alloc_psum_tensor(f"ps{i}", [128, 512], F32) for i in range(8)]
    R_ps = [Res(f"ps{i}") for i in range(8)]

    R_H = Res("H")
    R_wsl = [Res() for _ in range(4)]
    R_big = [Res(f"big{i}") for i in range(NBIG)]
    R_flex = [Res("flex")]
    R_flex.append(Res("flexjunk", parent=R_flex[0]))
    R_par = Res("params")
    R_mod = Res("mod")
    R_small = Res("small")
    R_lgt = Res("lgt")
    R_stl = Res("stl")
    R_rtab = Res("rtab")
    R_sst = Res("sstate")
    R_x1 = Res("x1d")
    R_x2 = Res("x2d")
    R_yT = Res("yTd")
    R_out = [Res("yout"), Res("olru"), Res("oret")]

    S_par = tk.newsem("par")
    S_w = [tk.newsem("w") for _ in range(4)]
    S_bl = [tk.newsem("bl") for _ in range(NBIG)]
    S_bs = [tk.newsem("bs") for _ in range(NBIG)]
    S_flex = [tk.newsem("fx") for _ in range(2)]
    S_H = tk.newsem("hld")
    S_lruw = tk.newsem("lruw")
    S_wo = [tk.newsem("wo") for _ in range(8)]
    S_sst = tk.newsem("sst")
    S_o = tk.newsem("ost")
    R_at = [Res(f"at{i}", parent=R_big[10]) for i in range(8)]
    R_modab = [[Res(f"modab{l}{i}") for i in range(2)] for l in range(2)]
    U2 = [big[2], u2t]
    R_u2 = [R_big[2], Res("u2t")]
    R_ss = [[[Res(f"ss{a}{b}{c}", parent=R_big[3]) for c in range(2)] for b in range(2)] for a in range(4)]
    S_ost = [[tk.newsem("ost") for _ in range(2)] for _ in range(4)]
    S_sst2 = [tk.newsem("sst2") for _ in range(2)]
    R_ac = [Res(f"ac{c}", parent=R_big[3]) for c in range(4)]
    R_sgc = [[Res(f"sg{p}{c}", parent=R_big[1 if p == 0 else 10]) for c in range(4)] for p in range(2)]
    R_bc = [Res(f"bc{c}", parent=R_big[4]) for c in range(4)]
    R_hc = [[Res(f"hc{d}{c}", parent=R_big[5 + d]) for c in range(4)] for d in range(2)]
    R_xc = [[Res(f"xc{p}{c}", parent=R_big[0 if p == 0 else 7]) for c in range(4)] for p in range(2)]
    R_ot = [Res(f"ot{i}", parent=R_big[10]) for i in range(3)]

    cnt = {"w": 0, "ps": 0, "ld": 0, "st": 0, "acc": 0}
    rot = {"base": 3, "n": NROT}

    def pf(name, k=None):
        c0, n = PF[name]
        if k is None:
            return pfm[:, c0:c0 + n]
        return pfm[:, c0 + k:c0 + k + 1]

    C_ID, C_PERM, C_MF, C_MB, C_EQF, C_EQB, C_ONES = (cst[:, i * 128:(i + 1) * 128] for i in range(7))
    c_ekf, c_ekb, c_128, c_one, c_eps = (cst[:, 896 + i:897 + i] for i in range(5))

    for dst, src in ((pfm[:], pfm_d), (cst[:], cst_d), (identb[:], identb_d), (pc[:], pcore),
                     (dft256[:, 0], dftC256_d.rearrange("(c p) k -> p c k", p=128)),
                     (dft256[:, 1], dftS256_d.rearrange("(c p) k -> p c k", p=128)),
                     (dftcc[:], dftCC_d.rearrange("(c p) k -> p c k", p=128))):
        dma("sp", S_par, dst, src, writes=(), pw=(R_par,))
    R_wrb = R_big[9]
    R_sbf = R_big[4]

    def load_lru_w():
        dma("pool", S_lruw, wrb, lru_wr.rearrange("d h i j -> i (d h) j"), writes=(R_wrb,))
        dma("pool", S_lruw, wib, lru_wi.rearrange("d h i j -> i (d h) j"), pw=(R_wrb,))

    def next_ps():
        i = rot["base"] + cnt["ps"] % rot["n"]
        cnt["ps"] += 1
        return ps[i], R_ps[i]

    def next_acc():
        i = cnt["acc"] % 2
        cnt["acc"] += 1
        return ps[i], R_ps[i]

    def load_w(w2d, col0):
        i = cnt["w"] % 4
        cnt["w"] += 1
        dma("pool", S_w[i], wsl[i][:], w2d[:, col0:col0 + 128].rearrange("(k p) c -> p k c", p=128),
            writes=(R_wsl[i],))
        return wsl[i], R_wsl[i]

    op("dve", lambda e: e.memset(onesb[:], 1.0 / 128.0), pw=(R_par,))
    op("act", lambda e: e.activation(scond[:].rearrange("p k r -> p (k r)"), pc[:, 0:32], AF.Silu),
       reads=(R_par,), writes=(R_mod,))
    op("act", lambda e: e.activation(smallf[:, 0:16], pf("rdec"), AF.Exp, scale=-1.0),
       reads=(R_par,), writes=(R_small,))
    op("act", lambda e: e.activation(lgt[:, 16:32], smallf[:, 0:16], AF.Ln, bias=c_one),
       reads=(R_small, R_par), pw=(R_lgt,))
    op("dve", lambda e: e.tensor_scalar(lgt[:, 0:16], lgt[:, 16:32], -1.0, None, ALU.mult),
       reads=(R_lgt,), pw=(R_lgt,))
    for d in range(2):
        op("act", lambda e, d=d: e.activation(smallf[:, 16 + 8 * d:24 + 8 * d], pf(f"lam{d}"), AF.Exp, scale=-1.0),
           reads=(R_par,), pw=(R_small,))
    op("act", lambda e: e.activation(smallf[:, 32:48], smallf[:, 16:32], AF.Ln, bias=c_one),
       reads=(R_small, R_par), pw=(R_small,))
    op("dve", lambda e: e.tensor_scalar(lgt[:, 32:48], smallf[:, 32:48], -8.0, None, ALU.mult),
       reads=(R_small,), pw=(R_lgt,))
    op("dve", lambda e: e.tensor_scalar(lgt[:, 48:64], smallf[:, 32:48], -16.0, None, ALU.mult),
       reads=(R_small,), pw=(R_lgt,))

    R_hb = Res("hb")
    for d in range(2):
        op("dve", lambda e, d=d: e.tensor_scalar(hb[:, d * 8:d * 8 + 8], pf(f"br{d}"), 0.5, None, ALU.mult),
           reads=(R_par,), pw=(R_hb,))
        op("dve", lambda e, d=d: e.tensor_scalar(hb[:, 16 + d * 8:24 + d * 8], pf(f"bi{d}"), 0.5, None, ALU.mult),
           reads=(R_par,), pw=(R_hb,))
    op("dve", lambda e: e.tensor_scalar(hb[:, 32:48], lgt[:, 32:48], 0.5, None, ALU.mult), reads=(R_lgt,), pw=(R_hb,))

    R_scond = R_mod

    def ada_pieces(l, bank, ocs, parts, col0=0):
        pt, rp = ps[bank], R_ps[bank]
        ptv = pt[:, col0:col0 + 96].rearrange("p (o r) -> p o r", r=2)
        pieces = []

        def chunk(oc, first):
            slot, rs = load_w(ada_w[l], oc * 128)
            grp("pe", [(lambda e, k=k: e.matmul(ptv[:, oc, :], slot[:, k, :], scond[:, k, :],
                                                start=(k == 0), stop=(k == 15))) for k in range(16)],
                reads=(rs, R_scond), writes=((rp,) if first else ()), pw=(() if first else (rp,)))

        def epilogue():
            for r in range(2):
                if "ab" in parts:
                    op("dve", lambda e, r=r: e.tensor_tensor(smallf[:, 0:16], ptv[:, 16:32, r], pf(f"absc{l}"), ALU.add),
                       reads=(rp, R_par), writes=(R_small,))
                    op("dve", lambda e, r=r: e.scalar_tensor_tensor(modAB[:, l, r, 0, :], smallf[:, 0:16], 1.0, pf(f"ng{l}"),
                                                                    ALU.add, ALU.mult),
                       reads=(R_small, R_par), pw=(R_modab[l][0],))
                    op("dve", lambda e, r=r: e.tensor_tensor(modAB[:, l, r, 1, :], ptv[:, 0:16, r], pf(f"absh{l}"), ALU.add),
                       reads=(rp, R_par), pw=(R_modab[l][0],))
                if "g" in parts:
                    op("dve", lambda e, r=r: e.tensor_tensor(modAB[:, l, r, 2, :], ptv[:, 32:48, r], pf(f"abg{l}"), ALU.add),
                       reads=(rp, R_par), pw=(R_modab[l][1],))
        for n_, oc in enumerate(ocs):
            pieces.append(lambda oc=oc, first=(n_ == 0 and (ocs[0] == 0 or col0 > 0)): chunk(oc, first))
        pieces.append(epilogue)
        return pieces

    bg = []

    def bg_step(n):
        for _ in range(n):
            if bg:
                bg.pop(0)()

    def phase_a(l, G, xsrc, rx):
        nt = min(G.T // 128, PA_TILES)
        for tt in range(nt):
            i = cnt["ld"] % 2
            cnt["ld"] += 1
            xt, rxt = big[i], R_big[i]
            xn, rxn = big[2][:].bitcast(BF16), R_big[2]
            junk, rj = big[3], R_big[3]
            dma("sp", S_bl[i], xt[:], xsrc[G.t0 + tt * 128:G.t0 + (tt + 1) * 128, :], reads=(rx,), writes=(rxt,))
            op("dve", lambda e: e.memset(smallf[:, 48:49], 0.0), writes=(R_small,))
            op("act", lambda e: e.activation(junk[:], xt[:], AF.Square, accum_out=smallf[:, 48:49]),
               reads=(rxt, R_small), writes=(rj,), pw=(R_small,))
            op("act", lambda e: e.activation(smallf[:, 49:50], smallf[:, 48:49], AF.Sqrt, scale=1.0 / D, bias=c_eps),
               reads=(R_small, R_par), pw=(R_small,))
            op("dve", lambda e: e.reciprocal(smallf[:, 50:51], smallf[:, 49:50]),
               reads=(R_small,), pw=(R_small,))
            op("dve", lambda e: e.tensor_scalar(xn[:, 0:2048], xt[:], smallf[:, 50:51], None, ALU.mult),
               reads=(rxt, R_small), writes=(rxn,))
            for q in range(4):
                pt, rp = next_ps()
                grp("pe", [(lambda e, j=j: e.matmul(pt[:, j * 128:(j + 1) * 128],
                                                    xn[:, (q * 4 + j) * 128:(q * 4 + j + 1) * 128], identb[:],
                                                    start=True, stop=True))
                           for j in range(4)], reads=(rxn, R_par), writes=(rp,))
                for j in range(4):
                    k = q * 4 + j
                    dst = H[:, k, tt * 128:(tt + 1) * 128]
                    if q % 2 == 0:
                        op("act", lambda e, j=j, k=k, dst=dst: e.activation(
                            dst, pt[:, j * 128:(j + 1) * 128], AF.Identity,
                            scale=modAB[:, l, G.row, 0, k:k + 1], bias=modAB[:, l, G.row, 1, k:k + 1]),
                           reads=(rp, R_modab[l][0]), pw=(R_H,))
                    else:
                        op("dve", lambda e, j=j, k=k, dst=dst: e.tensor_scalar(
                            dst, pt[:, j * 128:(j + 1) * 128], modAB[:, l, G.row, 0, k:k + 1],
                            modAB[:, l, G.row, 1, k:k + 1], ALU.mult, ALU.add),
                           reads=(rp, R_modab[l][0]), pw=(R_H,))

    def proj(l, G, col0, evac):
        slot, rs = load_w(w_in[l], col0)
        for tt in range(G.T // 512):
            pt, rp = next_ps()
            grp("pe", [(lambda e, k=k: e.matmul(pt[:], slot[:, k, :], H[:, k, tt * 512:(tt + 1) * 512],
                                                start=(k == 0), stop=(k == 15))) for k in range(16)],
                reads=(rs, R_H), writes=(rp,))
            evac(tt, pt, rp)
        bg_step(BG_PER_PROJ)

    def sl(tt):
        return slice(tt * 512, (tt + 1) * 512)

    def seqv(ap, G, lo, hi):
        return ap[:, 0:G.T].rearrange("p (s t) -> p s t", s=G.nseq)[:, :, lo:hi]

    def store_y(G, g, ybf, ry, bi):
        dma("sp", S_bs[bi], yT_d[g, :, G.t0:G.t0 + G.T], ybf, reads=(ry,), pw=(R_yT,))

    def lru_bufs(h):
        p = h % 2
        return (big[0], R_big[0], big[1], R_big[1]) if p == 0 else (big[7], R_big[7], big[10], R_big[10])

    def lru_front(G, h):
        xa, rxa, sg, rsg = lru_bufs(h)
        proj(0, G, h * 128, lambda tt, pt, rp: op(
            "act", lambda e: e.activation(xa[:, sl(tt)], pt[:], AF.Copy), reads=(rp,), pw=(rxa,)))
        def ev_sg(tt, pt, rp):
            op("act", lambda e: e.activation(sg[:, sl(tt)], pt[:], AF.Tanh, scale=0.5), reads=(rp,), writes=(R_sgc[h % 2][tt],))
            op("dve", lambda e: e.scalar_tensor_tensor(sg[:, sl(tt)], sg[:, sl(tt)], 1.0, pt[:], ALU.add, ALU.mult),
               reads=(rp,), writes=(R_sgc[h % 2][tt],))
        proj(0, G, W + h * 128, ev_sg)

    def lru_conv(G, h):
        T, L = G.T, G.L
        xa, rxa, _, _ = lru_bufs(h)
        u, ru = U2[h % 2], R_u2[h % 2]
        op("dve", lambda e: e.tensor_scalar(u[:, 0:T], xa[:, 0:T], pf("cw2", h), pf("cb", h), ALU.mult, ALU.add),
           reads=(rxa, R_par), writes=(ru,))
        for j, sh in ((0, -2), (1, -1), (3, 1)):
            if sh < 0:
                o_lo, o_hi, i_lo, i_hi = -sh, L, 0, L + sh
            else:
                o_lo, o_hi, i_lo, i_hi = 0, L - sh, sh, L
            op("dve", lambda e, j=j, a=(o_lo, o_hi, i_lo, i_hi): e.scalar_tensor_tensor(
                seqv(u, G, a[0], a[1]), seqv(xa, G, a[2], a[3]), pf(f"cw{j}", h), seqv(u, G, a[0], a[1]),
                ALU.mult, ALU.add), reads=(rxa, R_par), writes=(ru,))

    def lru_ub(G, h):
        ub, rub = big[8][:].bitcast(BF16), R_big[8]
        op("act", lambda e: e.activation(ub[:, 0:G.T], U2[h % 2][:, 0:G.T], AF.Copy), reads=(R_u2[h % 2],), writes=(rub,))

    def lru_back(G, h, hook, hook_late=None):
        T = G.T
        xa, rxa, sg, rsg = lru_bufs(h)
        u, ru = U2[h % 2], R_u2[h % 2]
        nxt = h + 1 if h < 7 else None
        A, rA = big[3], R_big[3]
        Bt, rB = big[4], R_big[4]
        Hf, rHf = big[5], R_big[5]
        Hb, rHb = big[6], R_big[6]
        ub, rub = big[8][:].bitcast(BF16), R_big[8]
        L = G.L
        if h == 0:
            lru_conv(G, 0)
            lru_ub(G, 0)
        hook()
        nck = T // 512
        for d in range(2):
            Hd, rHd = (Hf, rHf) if d == 0 else (Hb, rHb)
            dh = d * 8 + h
            hc = R_hc[d]
            xc = R_xc[h % 2]
            for c in range(nck):
                cs = sl(c)
                pt, rp = next_ps()
                op("pe", lambda e: e.matmul(pt[:], wrb[:, dh, :], ub[:, cs], start=True, stop=True),
                   reads=(rub, R_wrb), writes=(rp,))
                op("act", lambda e: e.activation(A[:, cs], pt[:], AF.Tanh, scale=0.5, bias=hb[:, dh:dh + 1]),
                   reads=(rp, R_hb), writes=(R_ac[c],))
                pt2, rp2 = next_ps()
                op("pe", lambda e: e.matmul(pt2[:], wib[:, dh, :], ub[:, cs], start=True, stop=True),
                   reads=(rub, R_wrb), writes=(rp2,))
                op("act", lambda e: e.activation(Bt[:, cs], pt2[:], AF.Tanh, scale=0.5, bias=hb[:, 16 + dh:17 + dh]),
                   reads=(rp2, R_hb), writes=(R_bc[c],))
                op("act", lambda e: e.activation(Hd[:, cs], A[:, cs], AF.Exp, scale=lgt[:, 32 + dh:33 + dh],
                                                 bias=lgt[:, 32 + dh:33 + dh]),
                   reads=(R_ac[c], R_lgt), writes=(hc[c],))
                op("act", lambda e: e.activation(xa[:, cs], A[:, cs], AF.Exp, scale=hb[:, 32 + dh:33 + dh],
                                                 bias=hb[:, 32 + dh:33 + dh]),
                   reads=(R_ac[c], R_hb), writes=(xc[c],))
            if d == 1 and nxt is not None:
                lru_ub(G, nxt)
            for c in range(nck):
                cs = sl(c)
                op("act", lambda e: e.activation(Hd[:, cs], Hd[:, cs], AF.Sqrt, scale=-1.0, bias=c_one),
                   reads=(R_par,), writes=(hc[c],))
            for c in range(nck):
                cs = sl(c)
                op("dve", lambda e: e.scalar_tensor_tensor(Bt[:, cs], Bt[:, cs], 1.0, u[:, cs], ALU.add, ALU.mult),
                   reads=(ru,), writes=(R_bc[c],))
                op("dve", lambda e: e.scalar_tensor_tensor(Bt[:, cs], Bt[:, cs], 0.5, Hd[:, cs], ALU.mult, ALU.mult),
                   reads=(hc[c],), writes=(R_bc[c],))
                if G.nseq > 1:
                    c0 = 0 if d == 0 else L - 1
                    av = xa[:, cs].rearrange("p (s t) -> p s t", t=L)[:, :, c0:c0 + 1]
                    op("dve", lambda e: e.memset(av, 0.0), writes=(xc[c],))
                if d == 0:
                    if G.nseq > 1:
                        init, rinit = 0.0, ()
                    elif c == 0:
                        init, rinit = pc[:, 32 + dh:33 + dh], (R_par,)
                    else:
                        init, rinit = Hd[:, c * 512 - 1:c * 512], (hc[c - 1],)
                    op("dve", lambda e: e.tensor_tensor_scan(Hd[:, cs], xa[:, cs], Bt[:, cs], init, ALU.mult, ALU.add),
                       reads=(xc[c], R_bc[c]) + rinit, writes=(hc[c],))
            if d == 0 and nxt is not None:
                lru_conv(G, nxt)
            if d == 1:
                for c in range(nck - 1, -1, -1):
                    cs = sl(c)
                    if G.nseq > 1:
                        init, rinit = 0.0, ()
                    elif c == nck - 1:
                        init, rinit = pc[:, 32 + dh:33 + dh], (R_par,)
                    else:
                        init, rinit = Hd[:, (c + 1) * 512:(c + 1) * 512 + 1], (hc[c + 1],)
                    op("dve", lambda e: e.tensor_tensor_scan(Hd[:, cs][:, ::-1], xa[:, cs][:, ::-1], Bt[:, cs][:, ::-1],
                                                             init, ALU.mult, ALU.add),
                       reads=(xc[c], R_bc[c]) + rinit, writes=(hc[c],))
            if G.nseq > 1:
                cf = L - 1 if d == 0 else 0
                dstv = stl[:].rearrange("p (s x) -> p s x", s=4)[:, :, d * 8 + h:d * 8 + h + 1]
                op("act", lambda e: e.activation(dstv, seqv(Hd, G, cf, cf + 1), AF.Copy), reads=(rHd,), pw=(R_stl,))
        if hook_late is not None:
            hook_late()
        op("dve", lambda e: e.tensor_tensor(Hf[:, 0:T], Hf[:, 0:T], Hb[:, 0:T], ALU.add), reads=(rHb,), writes=(rHf,))
        op("dve", lambda e: e.scalar_tensor_tensor(ub[:, 2048:2048 + T], Hf[:, 0:T], 0.5, sg[:, 0:T], ALU.mult, ALU.mult),
           reads=(rHf, rsg), writes=(rub,))
        store_y(G, h, ub[:, 2048:2048 + T], rub, 8)

    def ret_front(G, h):
        qf, rq = big[0], R_big[0]
        kf, rk = big[1], R_big[1]
        vf, r7 = big[7][:].bitcast(BF16), R_big[7]
        proj(0, G, 2 * W + h * 128, lambda tt, pt, rp: op(
            "act", lambda e: e.activation(qf[:, sl(tt)], pt[:], AF.Copy), reads=(rp,), pw=(rq,)))
        proj(0, G, 3 * W + h * 128, lambda tt, pt, rp: op(
            "act", lambda e: e.activation(kf[:, sl(tt)], pt[:], AF.Copy, scale=float(128.0 ** -0.5)),
            reads=(rp,), pw=(rk,)))
        proj(0, G, 4 * W + h * 128, lambda tt, pt, rp: op(
            "dve", lambda e: e.tensor_copy(vf[:, sl(tt)], pt[:]), reads=(rp,), pw=(r7,)))

    def ret_back(G, h, hook):
        T = G.T
        nch = T // 128
        cps = G.L // 128
        qf, rq = big[0], R_big[0]
        kf, rk = big[1], R_big[1]
        sgb, rsg = big[2], R_big[2]
        tmp, rtmp = big[3], R_big[3]
        tmp2, rtmp2 = big[4], R_big[4]
        b5, r5 = big[5][:].bitcast(BF16), R_big[5]
        b6, r6 = big[6][:].bitcast(BF16), R_big[6]
        b7, r7 = big[7][:].bitcast(BF16), R_big[7]
        b8, r8 = big[8][:].bitcast(BF16), R_big[8]
        b9, r9 = big[9][:].bitcast(BF16), R_big[9]
        b10, r10 = big[10][:].bitcast(BF16), R_big[10]
        qtf, qtb = b5[:, 0:2048], b5[:, 2048:4096]
        ktf, ktb = b6[:, 0:2048], b6[:, 2048:4096]
        vb, krb = b7[:, 0:2048], b7[:, 2048:4096]
        vt, kdf = b8[:, 0:2048], b8[:, 2048:4096]
        kdb, ybf = b9[:, 0:2048], b9[:, 2048:4096]
        ropeC, ropeS = flex[:, 0:2048], flex[:, 2048:4096]

        vf = big[7]
        proj(0, G, 5 * W + h * 128, lambda tt, pt, rp: op(
            "act", lambda e: e.activation(sgb[:, sl(tt)], pt[:], AF.Silu), reads=(rp,), pw=(rsg,)))
        for d in range(2):
            dh = d * 8 + h
            eq = C_EQF if d == 0 else C_EQB
            op("act", lambda e, d=d, dh=dh, eq=eq: e.activation(rtab[:, d, :], eq, AF.Exp, scale=lgt[:, dh:dh + 1]),
               reads=(R_par, R_lgt), writes=(R_rtab,) if d == 0 else (), pw=() if d == 0 else (R_rtab,))
            op("act", lambda e, d=d, dh=dh, eq=eq: e.activation(rtab[:, 2 + d, :], eq, AF.Exp,
                                                                scale=lgt[:, 16 + dh:17 + dh]),
               reads=(R_par, R_lgt), pw=(R_rtab,))
            ek = c_ekf if d == 0 else c_ekb
            op("act", lambda e, d=d, dh=dh, ek=ek: e.activation(smallf[:, 52 + d:53 + d], ek, AF.Exp,
                                                                scale=lgt[:, dh:dh + 1]),
               reads=(R_par, R_lgt), writes=(R_small,) if d == 0 else (), pw=() if d == 0 else (R_small,))
            op("act", lambda e, d=d, dh=dh: e.activation(smallf[:, 54 + d:55 + d], c_128, AF.Exp,
                                                         scale=lgt[:, dh:dh + 1]),
               reads=(R_par, R_lgt), pw=(R_small,))
        if G.rope:
            for src, rs_ in ((qf, rq), (kf, rk)):
                for tt in range(T // 512):
                    pt, rp = next_ps()
                    op("pe", lambda e, src=src: e.matmul(pt[:], C_PERM, src[:, sl(tt)], start=True, stop=True),
                       reads=(rs_, R_par), writes=(rp,))
                    op("dve", lambda e: e.tensor_tensor(tmp[:, sl(tt)], pt[:], ropeS[:, sl(tt)], ALU.mult),
                       reads=(rp, R_flex[0]), pw=(rtmp,))
                op("dve", lambda e, src=src: e.tensor_tensor(src[:, 0:T], src[:, 0:T], ropeC[:, 0:T], ALU.mult),
                   reads=(R_flex[0],), writes=(rs_,))
                op("dve", lambda e, src=src: e.tensor_tensor(src[:, 0:T], src[:, 0:T], tmp[:, 0:T], ALU.add),
                   reads=(rtmp,), writes=(rs_,))

        def chv(ap):
            return ap[:, 0:T].rearrange("p (c i) -> p c i", i=128)

        def rowt(i):
            return rtab[:, i:i + 1, :].broadcast_to([128, nch, 128])
        op("dve", lambda e: e.tensor_tensor(chv(qtf), chv(qf), rowt(0), ALU.mult), reads=(rq, R_rtab), pw=(r5,))
        op("dve", lambda e: e.tensor_tensor(chv(qtb), chv(qf), rowt(1), ALU.mult), reads=(rq, R_rtab), pw=(r5,))
        op("dve", lambda e: e.tensor_tensor(chv(ktf), chv(kf), rowt(2), ALU.mult), reads=(rk, R_rtab), pw=(r6,))
        op("dve", lambda e: e.tensor_tensor(chv(ktb), chv(kf), rowt(3), ALU.mult), reads=(rk, R_rtab), pw=(r6,))
        op("act", lambda e: e.activation(krb[:, 0:T], kf[:, 0:T], AF.Copy), reads=(rk,), pw=(r7,))
        for c0 in range(0, nch, 4):
            for which in range(2):
                pt, rp = next_ps()
                src_, rsrc = (vb, r7) if which == 0 else (krb, r7)
                grp("pe", [(lambda e, j=j, src_=src_: e.matmul(pt[:, j * 128:(j + 1) * 128],
                                                               src_[:, (c0 + j) * 128:(c0 + j + 1) * 128], identb[:],
                                                               start=True, stop=True))
                           for j in range(4)], reads=(rsrc, R_par), writes=(rp,))
                if which == 0:
                    op("act", lambda e: e.activation(vt[:, c0 * 128:(c0 + 4) * 128], pt[:], AF.Copy),
                       reads=(rp,), pw=(r8,))
                else:
                    op("act", lambda e: e.activation(kdf[:, c0 * 128:(c0 + 4) * 128], pt[:], AF.Copy,
                                                     scale=smallf[:, 52:53]), reads=(rp, R_small), pw=(r8,))
                    op("act", lambda e: e.activation(kdb[:, c0 * 128:(c0 + 4) * 128], pt[:], AF.Copy,
                                                     scale=smallf[:, 53:54]), reads=(rp, R_small), pw=(r9,))
        hook()
        stt_ = big[3]

        def sview(s_, d_, slot):
            o_ = ((s_ * 2 + d_) * 2 + slot) * 128
            return stt_[:, o_:o_ + 128]
        if G.nseq == 1:
            for d in range(2):
                dma("sp", S_sst2[d], sview(0, d, 0), st_ret[d, h], writes=(R_ss[0][d][0],))
        else:
            op("dve", lambda e: e.memset(stt_[:], 0.0), writes=(R_big[3],))
        cur = {}
        for step in range(cps):
            for s_ in range(G.nseq):
                for d in range(2):
                    cu = cur.get((s_, d), 0)
                    ci = step if d == 0 else cps - 1 - step
                    c = s_ * cps + ci
                    kd = kdf if d == 0 else kdb
                    op("act", lambda e, d=d, c=c, s_=s_, cu=cu: e.activation(sbf[:, d, c, :], sview(s_, d, cu), AF.Copy),
                       reads=(R_ss[s_][d][cu],), pw=(R_sbf,))
                    pt, rp = next_ps()
                    op("pe", lambda e, c=c, kd=kd: e.matmul(pt[:, 0:128], kd[:, c * 128:(c + 1) * 128],
                                                            vt[:, c * 128:(c + 1) * 128], start=True, stop=True),
                       reads=(r8, r9), writes=(rp,))
                    op("dve", lambda e, d=d, s_=s_, cu=cu: e.scalar_tensor_tensor(
                        sview(s_, d, 1 - cu), sview(s_, d, cu), smallf[:, 54 + d:55 + d], pt[:, 0:128],
                        ALU.mult, ALU.add), reads=(rp, R_small, R_ss[s_][d][cu]), writes=(R_ss[s_][d][1 - cu],))
                    cur[(s_, d)] = 1 - cu
        if G.nseq > 1:
            for s_ in range(G.nseq):
                for d in range(2):
                    cu = cur[(s_, d)]
                    dma("sp", S_ost[s_][d], oret[s_, d, h], sview(s_, d, cu), reads=(R_ss[s_][d][cu],), pw=(R_out[2],))
        at = b10
        for c0 in range(0, nch, 4):
            po, rpo = next_acc()
            for j in range(4):
                c = c0 + j
                cs_ = slice(c * 128, (c + 1) * 128)
                ats = []
                atr = []
                for d in range(2):
                    kt_, qt_ = (ktf, qtf) if d == 0 else (ktb, qtb)
                    pt, rp = next_ps()
                    op("pe", lambda e, kt_=kt_, qt_=qt_: e.matmul(pt[:, 0:128], kt_[:, cs_], qt_[:, cs_],
                                                                  start=True, stop=True),
                       reads=(r5, r6), writes=(rp,))
                    ai = (cnt["st"] % 8)
                    cnt["st"] += 1
                    a_ap = at[:, ai * 128:(ai + 1) * 128]
                    op("dve", lambda e, d=d, a_ap=a_ap: e.tensor_tensor(a_ap, pt[:, 0:128], C_MF if d == 0 else C_MB,
                                                                        ALU.mult),
                       reads=(rp, R_par), writes=(R_at[ai],))
                    ats.append(a_ap)
                    atr.append(R_at[ai])
                grp("pe", [
                    lambda e: e.matmul(po[:, j * 128:(j + 1) * 128], vt[:, cs_], ats[0], start=True, stop=False),
                    lambda e: e.matmul(po[:, j * 128:(j + 1) * 128], vt[:, cs_], ats[1], start=False, stop=False),
                    lambda e: e.matmul(po[:, j * 128:(j + 1) * 128], sbf[:, 0, c, :], qtf[:, cs_], start=False, stop=False),
                    lambda e: e.matmul(po[:, j * 128:(j + 1) * 128], sbf[:, 1, c, :], qtb[:, cs_], start=False, stop=True),
                ], reads=(r8, atr[0], atr[1], R_sbf, r5), writes=((rpo,) if j == 0 else ()), pw=(() if j == 0 else (rpo,)))
            tsl = slice(c0 * 128, (c0 + 4) * 128)
            oi = (c0 // 4) % 2
            o_t, ro = big[10][:, 512 + oi * 512:1024 + oi * 512], R_ot[oi]
            osq, rosq = big[10][:, 1536 + 0:2048], R_ot[2]
            op("act", lambda e: e.activation(o_t, po[:], AF.Copy), reads=(rpo,), writes=(ro,))
            osqb = osq.bitcast(BF16)[:, 0:512]
            op("act", lambda e: e.activation(osqb, po[:], AF.Square), reads=(rpo,), writes=(rosq,))
            pm, rpm = next_ps()
            op("pe", lambda e: e.matmul(pm[:], onesb[:], osqb, start=True, stop=True),
               reads=(rosq, R_par), writes=(rpm,))
            op("act", lambda e: e.activation(osq, pm[:], AF.Ln, bias=c_eps),
               reads=(rpm, R_par), writes=(rosq,))
            op("act", lambda e: e.activation(osq, osq, AF.Exp, scale=-0.5), writes=(rosq,))
            op("dve", lambda e: e.tensor_tensor(o_t, o_t, osq, ALU.mult),
               reads=(rosq,), writes=(ro,))
            op("dve", lambda e: e.scalar_tensor_tensor(ybf[:, tsl], sgb[:, tsl], pf("rng", h), o_t,
                                                       ALU.mult, ALU.mult),
               reads=(rsg, ro, R_par), pw=(r9,))
        store_y(G, 8 + h, ybf[:, 0:T], r9, 9)

    def fourier(G):
        T = G.T
        nt = T // 128
        ABt = [big[3 + i][:].bitcast(BF16) for i in range(8)]
        rAB = [R_big[3 + i] for i in range(8)]
        xcb, rxc = big[0][:].bitcast(BF16), R_big[0]

        def ab_ap(tq, g4):
            flat = (tq * 4 + g4) * 512
            return ABt[flat // 4096][:, flat % 4096: flat % 4096 + 512], rAB[flat // 4096]
        for g4 in range(4):
            for ch in range(2):
                proj(1, G, g4 * 256 + ch * 128, lambda tt, pt, rp, ch=ch: op(
                    "act", lambda e: e.activation(xcb[:, ch * 2048 + tt * 512: ch * 2048 + (tt + 1) * 512], pt[:], AF.Copy),
                    reads=(rp,), pw=(rxc,)))
            for tq in range(nt):
                pt, rp = next_ps()
                grp("pe", [(lambda e, ch=ch: e.matmul(pt[:], xcb[:, ch * 2048 + tq * 128: ch * 2048 + (tq + 1) * 128],
                                                      dftcc[:, ch, :], start=(ch == 0), stop=(ch == 1)))
                           for ch in range(2)], reads=(rxc, R_par), writes=(rp,))
                dst, rd = ab_ap(tq, g4)
                eng = "act" if tq % 2 == 0 else "dve"
                if eng == "act":
                    op("act", lambda e, dst=dst: e.activation(dst, pt[:], AF.Copy), reads=(rp,), pw=(rd,))
                else:
                    op("dve", lambda e, dst=dst: e.tensor_copy(dst, pt[:]), reads=(rp,), pw=(rd,))
        L = G.L
        cps = L // 128
        KW = min(256, L)
        sgc, rsgc = big[1], R_big[1]
        ybf, rybf = big[2][:].bitcast(BF16), R_big[2]
        if G.nseq == 1:
            blocks = [(kt * 256, 256, [(0, kt)]) for kt in range(8)]
        else:
            blocks = [(0, T, [(s, 0) for s in range(G.nseq)])]
        tabC = flex[:].bitcast(BF16)[:, 0:4096].rearrange("p (c k) -> p c k", k=256)
        tabS = flex[:].bitcast(BF16)[:, 4096:8192].rearrange("p (c k) -> p c k", k=256)
        for (tok0, ntok, subs) in blocks:
            if G.nseq == 1:
                kt = subs[0][1]
                dma("sp", S_flex[0], tabC, dftC_d[:, kt * 256:(kt + 1) * 256].rearrange("(c p) k -> p c k", p=128),
                    writes=(R_flex[0],))
                dma("sp", S_flex[1], tabS, dftS_d[:, kt * 256:(kt + 1) * 256].rearrange("(c p) k -> p c k", p=128),
                    pw=(R_flex[0],))
            for c8 in range(8):
                g4, ch = c8 // 2, c8 % 2
                slot, rs = load_w(w_in[1], W + c8 * 128)
                bw = min(512, ntok)
                for t5 in range(ntok // bw):
                    pt, rp = next_ps()
                    tsl = slice(tok0 + t5 * bw, tok0 + (t5 + 1) * bw)
                    grp("pe", [(lambda e, k=k: e.matmul(pt[:, 0:bw], slot[:, k, :], H[:, k, tsl],
                                                        start=(k == 0), stop=(k == 15))) for k in range(16)],
                        reads=(rs, R_H), writes=(rp,))
                    op("act", lambda e: e.activation(sgc[:, tsl], pt[:, 0:bw], AF.Silu), reads=(rp,), pw=(rsgc,))
                for (s, kt) in subs:
                    pt, rp = next_ps()
                    mms = []
                    rr = set()
                    for part in range(2):
                        for tc_ in range(cps):
                            tq = s * cps + tc_
                            abp, rab = ab_ap(tq, g4)
                            rr.add(rab)
                            lhs = abp[:, part * 256 + ch * 128: part * 256 + (ch + 1) * 128]
                            if G.nseq == 1:
                                rhs = (tabC if part == 0 else tabS)[:, tc_, :]
                            else:
                                rhs = dft256[:, part, tc_, :]
                            first = (part == 0 and tc_ == 0)
                            last = (part == 1 and tc_ == cps - 1)
                            mms.append(lambda e, lhs=lhs, rhs=rhs, first=first, last=last: e.matmul(
                                pt[:, 0:KW], lhs, rhs, start=first, stop=last))
                    grp("pe", mms, reads=tuple(rr) + (R_flex[0], R_par), writes=(rp,))
                    o0 = s * L + kt * KW
                    op("dve", lambda e, o0=o0: e.tensor_tensor(ybf[:, o0:o0 + KW], pt[:, 0:KW], sgc[:, o0:o0 + KW], ALU.mult),
                       reads=(rp, rsgc), pw=(rybf,))
                    dma("sp", S_bs[2], yT_d[c8, :, G.t0 + o0:G.t0 + o0 + KW], ybf[:, o0:o0 + KW], reads=(rybf,), pw=(R_yT,))

    def sconv(G, c8):
        T, L = G.T, G.L
        cd, rcd = big[0], R_big[0]
        z, rz = big[1], R_big[1]
        cz, rcz = big[2], R_big[2]
        sg, rsg = big[3], R_big[3]
        ybf, ry = big[4][:].bitcast(BF16), R_big[4]
        proj(1, G, 3 * W + c8 * 128, lambda tt, pt, rp: op(
            "act", lambda e: e.activation(cd[:, sl(tt)], pt[:], AF.Copy), reads=(rp,), pw=(rcd,)))
        proj(1, G, 4 * W + c8 * 128, lambda tt, pt, rp: op(
            "dve", lambda e: e.tensor_tensor(z[:, sl(tt)], pt[:], cd[:, sl(tt)], ALU.mult), reads=(rp, rcd), pw=(rz,)))
        op("dve", lambda e: e.tensor_scalar(cz[:, 0:T], z[:, 0:T], pf("sw1", c8), None, ALU.mult),
           reads=(rz, R_par), writes=(rcz,))
        op("dve", lambda e: e.scalar_tensor_tensor(seqv(cz, G, 1, L), seqv(z, G, 0, L - 1), pf("sw0", c8),
                                                   seqv(cz, G, 1, L), ALU.mult, ALU.add),
           reads=(rz, R_par), writes=(rcz,))
        op("dve", lambda e: e.scalar_tensor_tensor(seqv(cz, G, 0, L - 1), seqv(z, G, 1, L), pf("sw2", c8),
                                                   seqv(cz, G, 0, L - 1), ALU.mult, ALU.add),
           reads=(rz, R_par), writes=(rcz,))
        proj(1, G, 5 * W + c8 * 128, lambda tt, pt, rp: op(
            "act", lambda e: e.activation(sg[:, sl(tt)], pt[:], AF.Silu), reads=(rp,), pw=(rsg,)))
        proj(1, G, 2 * W + c8 * 128, lambda tt, pt, rp: op(
            "dve", lambda e: e.tensor_tensor(cz[:, sl(tt)], pt[:], cz[:, sl(tt)], ALU.mult), reads=(rp,), writes=(rcz,)))
        op("dve", lambda e: e.tensor_tensor(ybf[:, 0:T], cz[:, 0:T], sg[:, 0:T], ALU.mult), reads=(rcz, rsg), writes=(ry,))
        store_y(G, 8 + c8, ybf[:, 0:T], ry, 4)

    def phase_c(l, G, xsrc, rx, xdst, rxd, last):
        T = G.T
        gate, rg = big[8], R_big[8]
        gexp, rgx = big[9], R_big[9]
        op("dve", lambda e: e.tensor_copy(gexp[:].rearrange("p (k m) -> p k m", m=128),
                                          modAB[:, l, G.row, 2, :].unsqueeze(2).broadcast_to([128, 16, 128])),
           reads=(R_modab[l][1],), writes=(rgx,))
        for c4 in range(4):
            pt, rp = next_ps()
            grp("pe", [(lambda e, j=j: e.matmul(pt[:, j * 128:(j + 1) * 128],
                                                gexp[:, (c4 * 4 + j) * 128:(c4 * 4 + j + 1) * 128], C_ID,
                                                start=True, stop=True)) for j in range(4)],
                reads=(rgx, R_par), writes=(rp,))
            op("act", lambda e: e.activation(gate[:, sl(c4)], pt[:], AF.Copy), reads=(rp,),
               writes=((rg,) if c4 == 0 else ()), pw=(() if c4 == 0 else (rg,)))
        for g in range(16):
            dma("sp", S_H, H[:, g, 0:T], yT_d[g, :, G.t0:G.t0 + T], reads=(R_yT,),
                writes=((R_H,) if g == 0 else ()), pw=(() if g == 0 else (R_H,)))
        Wo = [big[i][:].bitcast(BF16) for i in range(8)]
        for g in range(16):
            dma("pool", S_wo[g // 2], Wo[g // 2][:, (g % 2) * 2048:(g % 2 + 1) * 2048], w_out[l, g * 128:(g + 1) * 128, :],
                writes=((R_big[g // 2],) if g % 2 == 0 else ()), pw=(() if g % 2 == 0 else (R_big[g // 2],)))
        if last:
            fg = flex[:, 0:2048]
            junk = flex[:, 2048:3072].bitcast(BF16)
            dma("sp", S_flex[0], fg.rearrange("p (o n) -> p o n", o=1), fng_d.partition_broadcast(128),
                writes=(R_flex[0],))
        for tt in range(T // 128):
            i = cnt["ld"] % 2
            cnt["ld"] += 1
            xt, rxt = big[9 + i], R_big[9 + i]
            r0 = G.t0 + tt * 128
            dma("sp", S_bl[9 + i], xt[:], xsrc[r0:r0 + 128, :], reads=(rx,), writes=(rxt,))
            for dt_ in range(4):
                pt, rp = next_ps()
                grp("pe", [(lambda e, g=g: e.matmul(pt[:], H[:, g, tt * 128:(tt + 1) * 128],
                                                    Wo[g // 2][:, (g % 2) * 2048 + dt_ * 512:(g % 2) * 2048 + (dt_ + 1) * 512],
                                                    start=(g == 0), stop=(g == 15))) for g in range(16)],
                    reads=(R_H,) + tuple(R_big[0:8]), writes=(rp,))
                op("dve", lambda e: e.tensor_tensor(pt[:], pt[:], gate[:, sl(dt_)], ALU.mult),
                   reads=(rg,), writes=(rp,))
                op("dve", lambda e: e.tensor_tensor(xt[:, sl(dt_)], xt[:, sl(dt_)], pt[:], ALU.add),
                   reads=(rp,), writes=(rxt,))
            if last:
                sq = 56 + (tt % 2) * 3
                op("dve", lambda e: e.memset(smallf[:, sq:sq + 1], 0.0), writes=(R_small,))
                op("act", lambda e: e.activation(junk, xt[:], AF.Square, accum_out=smallf[:, sq:sq + 1]),
                   reads=(rxt, R_small), writes=(R_flex[1],), pw=(R_small,))
                op("act", lambda e: e.activation(smallf[:, sq + 1:sq + 2], smallf[:, sq:sq + 1], AF.Sqrt, scale=1.0 / D,
                                                 bias=c_eps), reads=(R_small, R_par), pw=(R_small,))
                op("dve", lambda e: e.reciprocal(smallf[:, sq + 2:sq + 3], smallf[:, sq + 1:sq + 2]),
                   reads=(R_small,), pw=(R_small,))
                op("dve", lambda e: e.scalar_tensor_tensor(xt[:], xt[:], smallf[:, sq + 2:sq + 3], fg, ALU.mult, ALU.mult),
                   reads=(R_small, R_flex[0]), writes=(rxt,))
                dma("sp", S_bs[9 + i], y_out[r0:r0 + 128, :], xt[:], reads=(rxt,), pw=(R_out[0],))
            else:
                dma("sp", S_bs[9 + i], xdst[r0:r0 + 128, :], xt[:], reads=(rxt,), pw=(rxd,))

    def final_norm():
        fg, rfg = big[2], R_big[2]
        junk, rj = big[3], R_big[3]
        dma("sp", S_bl[2], fg[:].rearrange("p (o n) -> p o n", o=1), fng_d.partition_broadcast(128), writes=(rfg,))
        for tt in range(TT // 128):
            i = cnt["ld"] % 2
            cnt["ld"] += 1
            xt, rxt = big[i], R_big[i]
            dma("sp", S_bl[i], xt[:], x2_d[tt * 128:(tt + 1) * 128, :], reads=(R_x2,), writes=(rxt,))
            op("dve", lambda e: e.memset(smallf[:, 48:49], 0.0), writes=(R_small,))
            op("act", lambda e: e.activation(junk[:], xt[:], AF.Square, accum_out=smallf[:, 48:49]),
               reads=(rxt, R_small), writes=(rj,), pw=(R_small,))
            op("act", lambda e: e.activation(smallf[:, 49:50], smallf[:, 48:49], AF.Sqrt, scale=1.0 / D, bias=c_eps),
               reads=(R_small, R_par), pw=(R_small,))
            op("dve", lambda e: e.reciprocal(smallf[:, 50:51], smallf[:, 49:50]),
               reads=(R_small,), pw=(R_small,))
            op("dve", lambda e: e.scalar_tensor_tensor(xt[:], xt[:], smallf[:, 50:51], fg[:], ALU.mult, ALU.mult),
               reads=(R_small, rfg), writes=(rxt,))
            dma("sp", S_bs[i], y_out[tt * 128:(tt + 1) * 128, :], xt[:], reads=(rxt,), pw=(R_out[0],))

    if stop_after == "SETUP":
        op("act", lambda e: e.activation(big[0][:, 0:64], lgt[:], AF.Copy), reads=(R_lgt,), writes=(R_big[0],))
        dma("sp", S_bs[0], y_out[0:128, 0:64], big[0][:, 0:64], reads=(R_big[0],), pw=(R_out[0],))
        tk.final_wait("sp", R_out)
        return nc
    for p_ in ada_pieces(0, 0, list(range(32)), ("ab",)):
        p_()
    bg_l1 = ada_pieces(1, 2, list(range(48)), ("ab", "g"))
    N_BG_L1 = len(bg_l1)
    bg_l0 = ada_pieces(0, 2, list(range(32, 48)), ("g",), col0=128) + bg_l1
    if stop_after == "ADA":
        op("act", lambda e: e.activation(big[0][:, 0:128], modAB[:].rearrange("p a b c d -> p (a b c d)"), AF.Copy),
           reads=(R_mod,), writes=(R_big[0],))
        dma("sp", S_bs[0], y_out[0:128, 0:128], big[0][:, 0:128], reads=(R_big[0],), pw=(R_out[0],))
        tk.final_wait("sp", R_out)
        return nc
    dma("sp", S_flex[0], flex[:, 0:2048], ropeC_d, writes=(R_flex[0],))
    dma("sp", S_flex[0], flex[:, 2048:4096], ropeS_d, pw=(R_flex[0],))
    for G in (GS, GP):
        phase_a(0, G, x_in, Res())
        if stop_after == "A0":
            break
        load_lru_w()
        if G is GS:
            bg.extend(bg_l0)
        lru_front(G, 0)
        for h in range(8):
            if h < 7:
                lru_back(G, h, lambda: lru_front(G, h + 1))
            else:
                lru_back(G, h, lambda: None, lambda: ret_front(G, 0))
        for h in range(8):
            if h < 7:
                ret_back(G, h, lambda: ret_front(G, h + 1))
            else:
                ret_back(G, h, lambda: None)
        if G is GS:
            bg_step(max(0, len(bg) - N_BG_L1))
        else:
            bg_step(len(bg))
        if stop_after == "B0":
            continue
        phase_c(0, G, x_in, Res(), x1_d, R_x1, False)
    if stop_after is None or stop_after in ("B0", "C0"):
        pt, rp = next_ps()
        op("pe", lambda e: e.matmul(pt[0:64, 0:128], stl[:], C_ID, start=True, stop=True),
           reads=(R_stl, R_par), writes=(rp,))
        op("act", lambda e: e.activation(big[0][0:64, 0:128], pt[0:64, 0:128], AF.Copy),
           reads=(rp,), writes=(R_big[0],))
        dma("sp", S_bs[0], olru, big[0][0:64, 0:128], reads=(R_big[0],), pw=(R_out[1],))
    if stop_after is None:
        rot["base"], rot["n"] = 0, 8
        for G in (GS, GP):
            phase_a(1, G, x1_d, R_x1)
            fourier(G)
            for c8 in range(8):
                sconv(G, c8)
            phase_c(1, G, x1_d, R_x1, x2_d, R_x2, True)
    tk.final_wait("sp", R_out + [R_x1, R_x2, R_yT])
    return nc


_CACHE = {}


def _prep_inputs(inp, core):
    b = core
    f32 = np.float32
    x = np.concatenate([inp["x_sample"][b], inp["x_prompt"][4 * b:4 * b + 4].reshape(TP, D)], axis=0)
    cond = np.stack([inp["c"][b], inp["c_ctx"]], axis=0)
    cond_fm = cond.reshape(2, 16, 128).transpose(2, 1, 0).reshape(128, 32)
    h0 = inp["state_lru"][b, 0].reshape(2, 8, 128).transpose(2, 0, 1).reshape(128, 16)
    pcore = np.ascontiguousarray(np.concatenate([cond_fm, h0], axis=1), dtype=f32)
    return {"x": np.ascontiguousarray(x, dtype=f32), "pcore": pcore,
            "st_ret": np.ascontiguousarray(inp["state_ret"][b, 0], dtype=f32)}


def kernel(**inputs):
    inp = {k: np.asarray(v) for k, v in inputs.items()}
    pfm = _pack_params(inp)
    consts = _host_consts()
    nc = build(pfm.shape[1])
    shared = {"pfm": pfm, "ada_w": np.ascontiguousarray(inp["ada_w"]), "ada_b": np.ascontiguousarray(inp["ada_b"]),
              "w_in": np.ascontiguousarray(inp["w_in"]), "w_out": np.ascontiguousarray(inp["w_out"]),
              "lru_w_r": np.ascontiguousarray(inp["lru_w_r"][0]), "lru_w_i": np.ascontiguousarray(inp["lru_w_i"][0]),
              "final_norm_g": np.ascontiguousarray(inp["final_norm_g"].reshape(1, D))}
    shared.update(consts)
    in_maps = []
    for c in range(NCORES):
        m = dict(shared)
        m.update(_prep_inputs(inp, c))
        in_maps.append(m)
    res = run_bass_kernel_spmd(nc, in_maps, core_ids=list(range(NCORES)))
    ys = np.stack([r["y"][0:TS] for r in res.results], axis=0)
    yp = np.concatenate([r["y"][TS:].reshape(4, 256, D) for r in res.results], axis=0)
    slru = np.concatenate([r["o_lru"].reshape(4, 1, 2, W) for r in res.results], axis=0)
    sret = np.concatenate([r["o_ret"].reshape(4, 1, 2, 8, 128, 128) for r in res.results], axis=0)
    return (yp.astype(np.float32), ys.astype(np.float32), slru.astype(np.float32), sret.astype(np.float32))
```

```python
import numpy as np
import ml_dtypes
import concourse.bass as bass
import concourse.mybir as mybir
from concourse.bass_utils import run_bass_kernel_spmd

F32 = mybir.dt.float32
BF16 = mybir.dt.bfloat16
AF = mybir.ActivationFunctionType
ALU = mybir.AluOpType

D = 2048
W = 1024
NCORES = 8
TS, TP = 2048, 1024
TT = TS + TP
EPS = 1e-6
NBIG = 11
PA_STEP = 99
EVAC_ACT_ONLY = True
PA_TILES = 999
N_LRU = 8
LRU_STEP = 99
NROT = 5
BG_PER_PROJ = 1
PA_NBUF = 2
EVAC_SKIP = ()
EVAC_DST0 = False
PA_SRCMOD = 0
N_RET = 8


class Res:
    __slots__ = ("w", "r", "name", "parent", "children")

    def __init__(self, name="", parent=None):
        self.w = {}
        self.r = {}
        self.name = name
        self.parent = parent
        self.children = []
        if parent is not None:
            parent.children.append(self)


class Sem:
    __slots__ = ("h", "n")

    def __init__(self, h):
        self.h = h
        self.n = 0


class Eng:
    def __init__(self, name, handle, sem):
        self.name = name
        self.h = handle
        self.sem = sem
        self.seen = {}


class TK:
    def __init__(self, nc):
        self.nc = nc
        self.sems = []
        self.eng = {}
        for name, h in (("pe", nc.tensor), ("act", nc.scalar), ("dve", nc.vector),
                        ("pool", nc.gpsimd), ("sp", nc.sync)):
            self.eng[name] = Eng(name, h, self.newsem(name))

    def newsem(self, name):
        s = Sem(self.nc.alloc_semaphore("s_" + name + str(len(self.sems))))
        self.sems.append(s)
        return s

    def _collect(self, reads, writes, pw):
        deps = {}

        def add(d):
            for s, v in d.items():
                if deps.get(s, 0) < v:
                    deps[s] = v
        def addw(w):
            add(w.w)
            add(w.r)
            if w.parent is not None:
                add(w.parent.w)
                add(w.parent.r)
            for c in w.children:
                add(c.w)
                add(c.r)
        for r in reads:
            add(r.w)
            if r.parent is not None:
                add(r.parent.w)
            for c in r.children:
                add(c.w)
        for w in writes:
            addw(w)
        for w in pw:
            if w.r or any(c.r or c.w for c in w.children):
                addw(w)
        return deps

    def _wait(self, e, deps, skip_self):
        for s, v in deps.items():
            if skip_self and s is e.sem:
                continue
            if e.seen.get(s, 0) >= v:
                continue
            e.h.wait_ge(s.h, v)
            e.seen[s] = v

    def _commit(self, ev_s, ev_v, reads, writes, pw):
        for r in reads:
            if r.r.get(ev_s, 0) < ev_v:
                r.r[ev_s] = ev_v
        for w in writes:
            w.w = {ev_s: ev_v}
            w.r = {}
            for c in w.children:
                c.w = {}
                c.r = {}
        for w in pw:
            if w.r or any(c.r or c.w for c in w.children):
                w.w = {ev_s: ev_v}
                w.r = {}
                for c in w.children:
                    c.w = {}
                    c.r = {}
            else:
                w.w[ev_s] = ev_v

    def op(self, eng, fn, reads=(), writes=(), pw=()):
        e = self.eng[eng]
        self._wait(e, self._collect(reads, writes, pw), skip_self=(eng == "pe"))
        ins = fn(e.h)
        e.sem.n += 1
        ins.then_inc(e.sem.h, 1)
        self._commit(e.sem, e.sem.n, reads, writes, pw)

    def group(self, eng, fns, reads=(), writes=(), pw=()):
        e = self.eng[eng]
        self._wait(e, self._collect(reads, writes, pw), skip_self=(eng == "pe"))
        ins = None
        for fn in fns:
            ins = fn(e.h)
        e.sem.n += 1
        ins.then_inc(e.sem.h, 1)
        self._commit(e.sem, e.sem.n, reads, writes, pw)

    def dma(self, q, sem, out, in_, reads=(), writes=(), pw=()):
        e = self.eng[q]
        self._wait(e, self._collect(reads, writes, pw), skip_self=False)
        ins = e.h.dma_start(out=out, in_=in_)
        sem.n += 16
        ins.then_inc(sem.h, 16)
        self._commit(sem, sem.n, reads, writes, pw)

    def final_wait(self, q, resources):
        e = self.eng[q]
        deps = self._collect((), resources, ())
        self._wait(e, deps, skip_self=False)


def _host_consts():
    c = {}
    f32 = np.float32
    idx = np.arange(128)
    ident = np.eye(128, dtype=f32)
    perm = np.zeros((128, 128), f32)
    perm[(idx + 64) % 128, idx] = 1.0
    maskf = (idx[None, :] >= idx[:, None]).astype(f32)
    maskb = (idx[:, None] >= idx[None, :]).astype(f32)
    eqf = np.broadcast_to((idx + 1.0)[None, :], (128, 128)).astype(f32)
    eqb = np.broadcast_to((128.0 - idx)[None, :], (128, 128)).astype(f32)
    onesm = np.full((128, 128), 1.0 / 128.0, f32)
    cols = np.zeros((128, 8), f32)
    cols[:, 0] = 127.0 - idx
    cols[:, 1] = idx
    cols[:, 2] = 128.0
    cols[:, 3] = 1.0
    cols[:, 4] = EPS
    c["cst"] = np.concatenate([ident, perm, maskf, maskb, eqf, eqb, onesm, cols], axis=1)
    c["identb"] = ident.astype(ml_dtypes.bfloat16)
    t = np.arange(2048)
    row = (t // 64).astype(np.float64)
    col = (t % 64).astype(np.float64)
    freqs = 10000.0 ** (-np.arange(32, dtype=np.float64) / 32)
    ang = np.concatenate([row[:, None] * freqs, col[:, None] * freqs], axis=-1)
    cs, sn = np.cos(ang).T, np.sin(ang).T
    c["ropeC"] = np.concatenate([cs, cs], 0).astype(f32)
    c["ropeS"] = np.concatenate([-sn, sn], 0).astype(f32)

    def dft(n):
        k = np.arange(n, dtype=np.float64)
        a = 2.0 * np.pi * np.outer(k, k) / n
        return np.cos(a) / np.sqrt(n), np.sin(a) / np.sqrt(n)
    ct, st = dft(2048)
    c["dftC"] = ct.astype(ml_dtypes.bfloat16)
    c["dftS"] = (-st).astype(ml_dtypes.bfloat16)
    ct, st = dft(256)
    c["dftC256"] = ct.astype(ml_dtypes.bfloat16)
    c["dftS256"] = (-st).astype(ml_dtypes.bfloat16)
    c["dftCC"] = np.concatenate([ct, st], axis=1).astype(ml_dtypes.bfloat16)
    return c


PF = {}


def _pack_params(inp):
    cols = []
    off = [0]

    def add(name, vec):
        v = np.asarray(vec, np.float32).reshape(-1, 128).T
        PF[name] = (off[0], v.shape[1])
        off[0] += v.shape[1]
        cols.append(v)
    for l in range(2):
        add(f"ng{l}", inp["norm_g"][l])
        add(f"absh{l}", inp["ada_b"][l][0:D])
        add(f"absc{l}", inp["ada_b"][l][D:2 * D])
        add(f"abg{l}", inp["ada_b"][l][2 * D:3 * D])
    for j in range(4):
        add(f"cw{j}", inp["lru_conv_w"][0][j])
    add("cb", inp["lru_conv_b"][0])
    for d in range(2):
        add(f"lam{d}", inp["lru_lambda"][0][d])
        add(f"br{d}", inp["lru_b_r"][0][d])
        add(f"bi{d}", inp["lru_b_i"][0][d])
    add("rng", inp["ret_norm_g"][0])
    for j in range(3):
        add(f"sw{j}", inp["sconv_w"][0][j])
    rd = np.broadcast_to(np.asarray(inp["ret_decay"][0], np.float32).reshape(1, 16), (128, 16))
    PF["rdec"] = (off[0], 16)
    off[0] += 16
    cols.append(rd)
    return np.ascontiguousarray(np.concatenate(cols, axis=1))


class Group:
    def __init__(self, name, t0, T, L, nseq, row, rope):
        self.name, self.t0, self.T, self.L, self.nseq, self.row, self.rope = name, t0, T, L, nseq, row, rope


GS = Group("s", 0, TS, 2048, 1, 0, True)
GP = Group("p", TS, TP, 256, 4, 1, False)


def build(npf, debug=False, stop_after=None):
    nc = bass.Bass("TRN2", target_bir_lowering=False)
    tk = TK(nc)
    op, grp, dma = tk.op, tk.group, tk.dma

    def din(name, shape, dt=F32):
        return nc.dram_tensor(name, list(shape), dt, kind="ExternalInput").ap()

    def dout(name, shape, dt=F32):
        return nc.dram_tensor(name, list(shape), dt, kind="ExternalOutput").ap()

    def dscr(name, shape, dt=F32):
        return nc.dram_tensor(name, list(shape), dt, kind=("ExternalOutput" if debug else "Internal")).ap()

    x_in = din("x", [TT, D])
    pcore = din("pcore", [128, 48])
    st_ret = din("st_ret", [2, 8, 128, 128])
    pfm_d = din("pfm", [128, npf])
    cst_d = din("cst", [128, 7 * 128 + 8])
    identb_d = din("identb", [128, 128], BF16)
    ropeC_d = din("ropeC", [128, 2048])
    ropeS_d = din("ropeS", [128, 2048])
    dftC_d = din("dftC", [2048, 2048], BF16)
    dftS_d = din("dftS", [2048, 2048], BF16)
    dftC256_d = din("dftC256", [256, 256], BF16)
    dftS256_d = din("dftS256", [256, 256], BF16)
    dftCC_d = din("dftCC", [256, 512], BF16)
    ada_w = din("ada_w", [2, D, 3 * D])
    ada_b = din("ada_b", [2, 3 * D])
    w_in = din("w_in", [2, D, 6 * W])
    w_out = din("w_out", [2, D, D])
    lru_wr = din("lru_w_r", [2, 8, 128, 128])
    lru_wi = din("lru_w_i", [2, 8, 128, 128])
    fng_d = din("final_norm_g", [1, D])

    y_out = dout("y", [TT, D])
    olru = dout("o_lru", [64, 128])
    oret = dout("o_ret", [4, 2, 8, 128, 128])
    x1_d = dscr("x1", [TT, D])
    x2_d = dscr("x2", [TT, D])
    yT_d = dscr("yT", [16, 128, TT], BF16)

    def sb(name, shape, dt=F32):
        return nc.alloc_sbuf_tensor(name, list(shape), dt)

    H = sb("H", [128, 16, TS], BF16)
    wsl = [sb(f"wsl{i}", [128, 16, 128], BF16) for i in range(4)]
    big = [sb(f"big{i}", [128, 2048], F32) for i in range(NBIG)]
    flex = sb("flex", [128, 4096], F32)
    pfm = sb("pfm_sb", [128, npf])
    cst = sb("cst_sb", [128, 7 * 128 + 8])
    identb = sb("identb_sb", [128, 128], BF16)
    pc = sb("pc_sb", [128, 48])
    scond = sb("scond", [128, 16, 2], BF16)
    modAB = sb("modAB", [128, 2, 2, 3, 16])
    smallf = sb("smallf", [128, 64])
    lgt = sb("lgt", [128, 64])
    stl = sb("stl", [128, 64])
    pasm = sb("pasm", [128, 64])
    onesb = sb("onesb", [128, 128], BF16)
    u2t = sb("u2t", [128, 2048])
    hb = sb("hb", [128, 48])
    rtab = sb("rtab", [128, 4, 128])
    wrb = big[9][:].bitcast(BF16)[:, 0:2048].rearrange("p (a b) -> p a b", b=128)
    wib = big[9][:].bitcast(BF16)[:, 2048:4096].rearrange("p (a b) -> p a b", b=128)
    sbf = big[4][:].bitcast(BF16).rearrange("p (d c e) -> p d c e", d=2, c=16)
    dft256 = sb("dft256", [128, 2, 2, 256], BF16)
    dftcc = sb("dftcc", [128, 2, 512], BF16)

    ps = [nc.alloc_psum_tensor(f"ps{i}", [128, 512], F32) for i in range(8)]
    R_ps = [Res(f"ps{i}") for i in range(8)]

    R_H = Res("H")
    R_wsl = [Res() for _ in range(4)]
    R_big = [Res(f"big{i}") for i in range(NBIG)]
    R_flex = [Res("flex")]
    R_flex.append(Res("flexjunk", parent=R_flex[0]))
    R_par = Res("params")
    R_mod = Res("mod")
    R_small = Res("small")
    R_lgt = Res("lgt")
    R_stl = Res("stl")
    R_rtab = Res("rtab")
    R_sst = Res("sstate")
    R_x1 = Res("x1d")
    R_x2 = Res("x2d")
    R_yT = Res("yTd")
    R_out = [Res("yout"), Res("olru"), Res("oret")]

    S_par = tk.newsem("par")
    S_w = [tk.newsem("w") for _ in range(4)]
    S_bl = [tk.newsem("bl") for _ in range(NBIG)]
    S_bs = [tk.newsem("bs") for _ in range(NBIG)]
    S_flex = [tk.newsem("fx") for _ in range(2)]
    S_H = tk.newsem("hld")
    S_lruw = tk.newsem("lruw")
    S_wo = [tk.newsem("wo") for _ in range(8)]
    S_sst = tk.newsem("sst")
    S_o = tk.newsem("ost")
    R_at = [Res(f"at{i}", parent=R_big[10]) for i in range(8)]
    R_modab = [[Res(f"modab{l}{i}") for i in range(2)] for l in range(2)]
    R_pasm = [Res(f"pasm{i}") for i in range(16)]
    U2 = [big[2], u2t]
    R_u2 = [R_big[2], Res("u2t")]
    R_ss = [[[Res(f"ss{a}{b}{c}", parent=R_big[3]) for c in range(2)] for b in range(2)] for a in range(4)]
    S_ost = [[tk.newsem("ost") for _ in range(2)] for _ in range(4)]
    S_sst2 = [tk.newsem("sst2") for _ in range(2)]
    R_ac = [Res(f"ac{c}", parent=R_big[3]) for c in range(4)]
    R_sgc = [[Res(f"sg{p}{c}", parent=R_big[1 if p == 0 else 10]) for c in range(4)] for p in range(2)]
    R_bc = [Res(f"bc{c}", parent=R_big[4]) for c in range(4)]
    R_hc = [[Res(f"hc{d}{c}", parent=R_big[5 + d]) for c in range(4)] for d in range(2)]
    R_xc = [[Res(f"xc{p}{c}", parent=R_big[0 if p == 0 else 7]) for c in range(4)] for p in range(2)]
    R_ot = [Res(f"ot{i}", parent=R_big[10]) for i in range(3)]

    cnt = {"w": 0, "ps": 0, "ld": 0, "st": 0, "acc": 0}
    rot = {"base": 3, "n": NROT}

    def pf(name, k=None):
        c0, n = PF[name]
        if k is None:
            return pfm[:, c0:c0 + n]
        return pfm[:, c0 + k:c0 + k + 1]

    C_ID, C_PERM, C_MF, C_MB, C_EQF, C_EQB, C_ONES = (cst[:, i * 128:(i + 1) * 128] for i in range(7))
    c_ekf, c_ekb, c_128, c_one, c_eps = (cst[:, 896 + i:897 + i] for i in range(5))

    for dst, src in ((pfm[:], pfm_d), (cst[:], cst_d), (identb[:], identb_d), (pc[:], pcore),
                     (dft256[:, 0], dftC256_d.rearrange("(c p) k -> p c k", p=128)),
                     (dft256[:, 1], dftS256_d.rearrange("(c p) k -> p c k", p=128)),
                     (dftcc[:], dftCC_d.rearrange("(c p) k -> p c k", p=128))):
        dma("sp", S_par, dst, src, writes=(), pw=(R_par,))
    R_wrb = R_big[9]
    R_sbf = R_big[4]

    def load_lru_w():
        dma("pool", S_lruw, wrb, lru_wr.rearrange("d h i j -> i (d h) j"), writes=(R_wrb,))
        dma("pool", S_lruw, wib, lru_wi.rearrange("d h i j -> i (d h) j"), pw=(R_wrb,))

    def next_ps():
        i = rot["base"] + cnt["ps"] % rot["n"]
        cnt["ps"] += 1
        return ps[i], R_ps[i]

    def next_acc():
        i = cnt["acc"] % 2
        cnt["acc"] += 1
        return ps[i], R_ps[i]

    def load_w(w2d, col0):
        i = cnt["w"] % 4
        cnt["w"] += 1
        dma("pool", S_w[i], wsl[i][:], w2d[:, col0:col0 + 128].rearrange("(k p) c -> p k c", p=128),
            writes=(R_wsl[i],))
        return wsl[i], R_wsl[i]

    op("dve", lambda e: e.memset(onesb[:], 1.0 / 128.0), pw=(R_par,))
    op("act", lambda e: e.activation(scond[:].rearrange("p k r -> p (k r)"), pc[:, 0:32], AF.Silu),
       reads=(R_par,), writes=(R_mod,))
    op("act", lambda e: e.activation(smallf[:, 0:16], pf("rdec"), AF.Exp, scale=-1.0),
       reads=(R_par,), writes=(R_small,))
    op("act", lambda e: e.activation(lgt[:, 16:32], smallf[:, 0:16], AF.Ln, bias=c_one),
       reads=(R_small, R_par), pw=(R_lgt,))
    op("dve", lambda e: e.tensor_scalar(lgt[:, 0:16], lgt[:, 16:32], -1.0, None, ALU.mult),
       reads=(R_lgt,), pw=(R_lgt,))
    for d in range(2):
        op("act", lambda e, d=d: e.activation(smallf[:, 16 + 8 * d:24 + 8 * d], pf(f"lam{d}"), AF.Exp, scale=-1.0),
           reads=(R_par,), pw=(R_small,))
    op("act", lambda e: e.activation(smallf[:, 32:48], smallf[:, 16:32], AF.Ln, bias=c_one),
       reads=(R_small, R_par), pw=(R_small,))
    op("dve", lambda e: e.tensor_scalar(lgt[:, 32:48], smallf[:, 32:48], -8.0, None, ALU.mult),
       reads=(R_small,), pw=(R_lgt,))
    op("dve", lambda e: e.tensor_scalar(lgt[:, 48:64], smallf[:, 32:48], -16.0, None, ALU.mult),
       reads=(R_small,), pw=(R_lgt,))

    R_hb = Res("hb")
    for d in range(2):
        op("dve", lambda e, d=d: e.tensor_scalar(hb[:, d * 8:d * 8 + 8], pf(f"br{d}"), 0.5, None, ALU.mult),
           reads=(R_par,), pw=(R_hb,))
        op("dve", lambda e, d=d: e.tensor_scalar(hb[:, 16 + d * 8:24 + d * 8], pf(f"bi{d}"), 0.5, None, ALU.mult),
           reads=(R_par,), pw=(R_hb,))
    op("dve", lambda e: e.tensor_scalar(hb[:, 32:48], lgt[:, 32:48], 0.5, None, ALU.mult), reads=(R_lgt,), pw=(R_hb,))

    R_scond = R_mod

    def ada_pieces(l, bank, ocs, parts, col0=0):
        pt, rp = ps[bank], R_ps[bank]
        ptv = pt[:, col0:col0 + 96].rearrange("p (o r) -> p o r", r=2)
        pieces = []

        def chunk(oc, first):
            slot, rs = load_w(ada_w[l], oc * 128)
            grp("pe", [(lambda e, k=k: e.matmul(ptv[:, oc, :], slot[:, k, :], scond[:, k, :],
                                                start=(k == 0), stop=(k == 15))) for k in range(16)],
                reads=(rs, R_scond), writes=((rp,) if first else ()), pw=(() if first else (rp,)))

        def epilogue():
            for r in range(2):
                if "ab" in parts:
                    op("dve", lambda e, r=r: e.tensor_tensor(smallf[:, 0:16], ptv[:, 16:32, r], pf(f"absc{l}"), ALU.add),
                       reads=(rp, R_par), writes=(R_small,))
                    op("dve", lambda e, r=r: e.scalar_tensor_tensor(modAB[:, l, r, 0, :], smallf[:, 0:16], 1.0, pf(f"ng{l}"),
                                                                    ALU.add, ALU.mult),
                       reads=(R_small, R_par), pw=(R_modab[l][0],))
                    op("dve", lambda e, r=r: e.tensor_tensor(modAB[:, l, r, 1, :], ptv[:, 0:16, r], pf(f"absh{l}"), ALU.add),
                       reads=(rp, R_par), pw=(R_modab[l][0],))
                if "g" in parts:
                    op("dve", lambda e, r=r: e.tensor_tensor(modAB[:, l, r, 2, :], ptv[:, 32:48, r], pf(f"abg{l}"), ALU.add),
                       reads=(rp, R_par), pw=(R_modab[l][1],))
        for n_, oc in enumerate(ocs):
            pieces.append(lambda oc=oc, first=(n_ == 0 and (ocs[0] == 0 or col0 > 0)): chunk(oc, first))
        pieces.append(epilogue)
        return pieces

    bg = []

    def bg_step(n):
        for _ in range(n):
            if bg:
                bg.pop(0)()

    def phase_a(l, G, xsrc, rx):
        nt = min(G.T // 128, PA_TILES)
        xbufs = (0, 1, 4, 5)
        xn, rxn = big[2][:].bitcast(BF16), R_big[2]
        junk, rj = big[3], R_big[3]
        op("dve", lambda e: e.memset(pasm[:], 0.0), writes=tuple(R_pasm))

        def stage1(tt):
            bi = xbufs[tt % 4]
            xt, rxt = big[bi], R_big[bi]
            sm, rsm = pasm[:, (tt % 16) * 4:(tt % 16) * 4 + 4], R_pasm[tt % 16]
            dma("sp", S_bl[bi], xt[:], xsrc[G.t0 + tt * 128:G.t0 + (tt + 1) * 128, :], reads=(rx,), writes=(rxt,))
            if tt >= 16:
                op("dve", lambda e: e.memset(sm[:, 0:1], 0.0), writes=(rsm,))
            op("act", lambda e: e.activation(junk[:], xt[:], AF.Square, accum_out=sm[:, 0:1]),
               reads=(rxt,), writes=(rj, rsm))
            op("act", lambda e: e.activation(sm[:, 1:2], sm[:, 0:1], AF.Sqrt, scale=1.0 / D, bias=c_eps),
               reads=(R_par,), writes=(rsm,))
            op("dve", lambda e: e.reciprocal(sm[:, 2:3], sm[:, 1:2]), writes=(rsm,))

        def stage2(tt):
            bi = xbufs[tt % 4]
            xt, rxt = big[bi], R_big[bi]
            sm, rsm = pasm[:, (tt % 16) * 4:(tt % 16) * 4 + 4], R_pasm[tt % 16]
            op("dve", lambda e: e.tensor_scalar(xn[:, 0:2048], xt[:], sm[:, 2:3], None, ALU.mult),
               reads=(rxt, rsm), writes=(rxn,))
            for q in range(4):
                pt, rp = next_ps()
                grp("pe", [(lambda e, j=j: e.matmul(pt[:, j * 128:(j + 1) * 128],
                                                    xn[:, (q * 4 + j) * 128:(q * 4 + j + 1) * 128], identb[:],
                                                    start=True, stop=True))
                           for j in range(4)], reads=(rxn, R_par), writes=(rp,))
                for j in range(4):
                    k = q * 4 + j
                    dst = H[:, k, tt * 128:(tt + 1) * 128]
                    if q % 2 == 0:
                        op("act", lambda e, j=j, k=k, dst=dst: e.activation(
                            dst, pt[:, j * 128:(j + 1) * 128], AF.Identity,
                            scale=modAB[:, l, G.row, 0, k:k + 1], bias=modAB[:, l, G.row, 1, k:k + 1]),
                           reads=(rp, R_modab[l][0]), pw=(R_H,))
                    else:
                        op("dve", lambda e, j=j, k=k, dst=dst: e.tensor_scalar(
                            dst, pt[:, j * 128:(j + 1) * 128], modAB[:, l, G.row, 0, k:k + 1],
                            modAB[:, l, G.row, 1, k:k + 1], ALU.mult, ALU.add),
                           reads=(rp, R_modab[l][0]), pw=(R_H,))
        stage1(0)
        for tt in range(nt):
            if tt + 1 < nt:
                stage1(tt + 1)
            stage2(tt)

    def proj(l, G, col0, evac):
        slot, rs = load_w(w_in[l], col0)
        for tt in range(G.T // 512):
            pt, rp = next_ps()
            grp("pe", [(lambda e, k=k: e.matmul(pt[:], slot[:, k, :], H[:, k, tt * 512:(tt + 1) * 512],
                                                start=(k == 0), stop=(k == 15))) for k in range(16)],
                reads=(rs, R_H), writes=(rp,))
            evac(tt, pt, rp)
        bg_step(BG_PER_PROJ)

    def sl(tt):
        return slice(tt * 512, (tt + 1) * 512)

    def seqv(ap, G, lo, hi):
        return ap[:, 0:G.T].rearrange("p (s t) -> p s t", s=G.nseq)[:, :, lo:hi]

    def store_y(G, g, ybf, ry, bi):
        dma("sp", S_bs[bi], yT_d[g, :, G.t0:G.t0 + G.T], ybf, reads=(ry,), pw=(R_yT,))

    def lru_bufs(h):
        p = h % 2
        return (big[0], R_big[0], big[1], R_big[1]) if p == 0 else (big[7], R_big[7], big[10], R_big[10])

    def lru_front(G, h):
        xa, rxa, sg, rsg = lru_bufs(h)
        proj(0, G, h * 128, lambda tt, pt, rp: op(
            "act", lambda e: e.activation(xa[:, sl(tt)], pt[:], AF.Copy), reads=(rp,), pw=(rxa,)))
        def ev_sg(tt, pt, rp):
            op("act", lambda e: e.activation(sg[:, sl(tt)], pt[:], AF.Tanh, scale=0.5), reads=(rp,), writes=(R_sgc[h % 2][tt],))
            op("dve", lambda e: e.scalar_tensor_tensor(sg[:, sl(tt)], sg[:, sl(tt)], 1.0, pt[:], ALU.add, ALU.mult),
               reads=(rp,), writes=(R_sgc[h % 2][tt],))
        proj(0, G, W + h * 128, ev_sg)

    def lru_conv(G, h):
        T, L = G.T, G.L
        xa, rxa, _, _ = lru_bufs(h)
        u, ru = U2[h % 2], R_u2[h % 2]
        op("dve", lambda e: e.tensor_scalar(u[:, 0:T], xa[:, 0:T], pf("cw2", h), pf("cb", h), ALU.mult, ALU.add),
           reads=(rxa, R_par), writes=(ru,))
        for j, sh in ((0, -2), (1, -1), (3, 1)):
            if sh < 0:
                o_lo, o_hi, i_lo, i_hi = -sh, L, 0, L + sh
            else:
                o_lo, o_hi, i_lo, i_hi = 0, L - sh, sh, L
            op("dve", lambda e, j=j, a=(o_lo, o_hi, i_lo, i_hi): e.scalar_tensor_tensor(
                seqv(u, G, a[0], a[1]), seqv(xa, G, a[2], a[3]), pf(f"cw{j}", h), seqv(u, G, a[0], a[1]),
                ALU.mult, ALU.add), reads=(rxa, R_par), writes=(ru,))

    def lru_ub(G, h):
        ub, rub = big[8][:].bitcast(BF16), R_big[8]
        op("act", lambda e: e.activation(ub[:, 0:G.T], U2[h % 2][:, 0:G.T], AF.Copy), reads=(R_u2[h % 2],), writes=(rub,))

    def lru_back(G, h, hook, hook_late=None):
        T = G.T
        xa, rxa, sg, rsg = lru_bufs(h)
        u, ru = U2[h % 2], R_u2[h % 2]
        nxt = h + 1 if h < 7 else None
        A, rA = big[3], R_big[3]
        Bt, rB = big[4], R_big[4]
        Hf, rHf = big[5], R_big[5]
        Hb, rHb = big[6], R_big[6]
        ub, rub = big[8][:].bitcast(BF16), R_big[8]
        L = G.L
        if h == 0:
            lru_conv(G, 0)
            lru_ub(G, 0)
        hook()
        nck = T // 512
        for d in range(2):
            Hd, rHd = (Hf, rHf) if d == 0 else (Hb, rHb)
            dh = d * 8 + h
            hc = R_hc[d]
            xc = R_xc[h % 2]
            for c in range(nck):
                cs = sl(c)
                pt, rp = next_ps()
                op("pe", lambda e: e.matmul(pt[:], wrb[:, dh, :], ub[:, cs], start=True, stop=True),
                   reads=(rub, R_wrb), writes=(rp,))
                op("act", lambda e: e.activation(A[:, cs], pt[:], AF.Tanh, scale=0.5, bias=hb[:, dh:dh + 1]),
                   reads=(rp, R_hb), writes=(R_ac[c],))
                pt2, rp2 = next_ps()
                op("pe", lambda e: e.matmul(pt2[:], wib[:, dh, :], ub[:, cs], start=True, stop=True),
                   reads=(rub, R_wrb), writes=(rp2,))
                op("act", lambda e: e.activation(Bt[:, cs], pt2[:], AF.Tanh, scale=0.5, bias=hb[:, 16 + dh:17 + dh]),
                   reads=(rp2, R_hb), writes=(R_bc[c],))
                op("act", lambda e: e.activation(Hd[:, cs], A[:, cs], AF.Exp, scale=lgt[:, 32 + dh:33 + dh],
                                                 bias=lgt[:, 32 + dh:33 + dh]),
                   reads=(R_ac[c], R_lgt), writes=(hc[c],))
                op("act", lambda e: e.activation(xa[:, cs], A[:, cs], AF.Exp, scale=hb[:, 32 + dh:33 + dh],
                                                 bias=hb[:, 32 + dh:33 + dh]),
                   reads=(R_ac[c], R_hb), writes=(xc[c],))
            if d == 1 and nxt is not None:
                lru_ub(G, nxt)
            for c in range(nck):
                cs = sl(c)
                op("act", lambda e: e.activation(Hd[:, cs], Hd[:, cs], AF.Sqrt, scale=-1.0, bias=c_one),
                   reads=(R_par,), writes=(hc[c],))
            for c in range(nck):
                cs = sl(c)
                op("dve", lambda e: e.scalar_tensor_tensor(Bt[:, cs], Bt[:, cs], 1.0, u[:, cs], ALU.add, ALU.mult),
                   reads=(ru,), writes=(R_bc[c],))
                op("dve", lambda e: e.scalar_tensor_tensor(Bt[:, cs], Bt[:, cs], 0.5, Hd[:, cs], ALU.mult, ALU.mult),
                   reads=(hc[c],), writes=(R_bc[c],))
                if G.nseq > 1:
                    c0 = 0 if d == 0 else L - 1
                    av = xa[:, cs].rearrange("p (s t) -> p s t", t=L)[:, :, c0:c0 + 1]
                    op("dve", lambda e: e.memset(av, 0.0), writes=(xc[c],))
                if d == 0:
                    if G.nseq > 1:
                        init, rinit = 0.0, ()
                    elif c == 0:
                        init, rinit = pc[:, 32 + dh:33 + dh], (R_par,)
                    else:
                        init, rinit = Hd[:, c * 512 - 1:c * 512], (hc[c - 1],)
                    op("dve", lambda e: e.tensor_tensor_scan(Hd[:, cs], xa[:, cs], Bt[:, cs], init, ALU.mult, ALU.add),
                       reads=(xc[c], R_bc[c]) + rinit, writes=(hc[c],))
            if d == 0 and nxt is not None:
                lru_conv(G, nxt)
            if d == 1:
                for c in range(nck - 1, -1, -1):
                    cs = sl(c)
                    if G.nseq > 1:
                        init, rinit = 0.0, ()
                    elif c == nck - 1:
                        init, rinit = pc[:, 32 + dh:33 + dh], (R_par,)
                    else:
                        init, rinit = Hd[:, (c + 1) * 512:(c + 1) * 512 + 1], (hc[c + 1],)
                    op("dve", lambda e: e.tensor_tensor_scan(Hd[:, cs][:, ::-1], xa[:, cs][:, ::-1], Bt[:, cs][:, ::-1],
                                                             init, ALU.mult, ALU.add),
                       reads=(xc[c], R_bc[c]) + rinit, writes=(hc[c],))
            if G.nseq > 1:
                cf = L - 1 if d == 0 else 0
                dstv = stl[:].rearrange("p (s x) -> p s x", s=4)[:, :, d * 8 + h:d * 8 + h + 1]
                op("act", lambda e: e.activation(dstv, seqv(Hd, G, cf, cf + 1), AF.Copy), reads=(rHd,), pw=(R_stl,))
        if hook_late is not None:
            hook_late()
        op("dve", lambda e: e.tensor_tensor(Hf[:, 0:T], Hf[:, 0:T], Hb[:, 0:T], ALU.add), reads=(rHb,), writes=(rHf,))
        op("dve", lambda e: e.scalar_tensor_tensor(ub[:, 2048:2048 + T], Hf[:, 0:T], 0.5, sg[:, 0:T], ALU.mult, ALU.mult),
           reads=(rHf, rsg), writes=(rub,))
        store_y(G, h, ub[:, 2048:2048 + T], rub, 8)

    def ret_front(G, h):
        qf, rq = big[0], R_big[0]
        kf, rk = big[1], R_big[1]
        vf, r7 = big[7][:].bitcast(BF16), R_big[7]
        proj(0, G, 2 * W + h * 128, lambda tt, pt, rp: op(
            "act", lambda e: e.activation(qf[:, sl(tt)], pt[:], AF.Copy), reads=(rp,), pw=(rq,)))
        proj(0, G, 3 * W + h * 128, lambda tt, pt, rp: op(
            "act", lambda e: e.activation(kf[:, sl(tt)], pt[:], AF.Copy, scale=float(128.0 ** -0.5)),
            reads=(rp,), pw=(rk,)))
        proj(0, G, 4 * W + h * 128, lambda tt, pt, rp: op(
            "dve", lambda e: e.tensor_copy(vf[:, sl(tt)], pt[:]), reads=(rp,), pw=(r7,)))

    def ret_back(G, h, hook):
        T = G.T
        nch = T // 128
        cps = G.L // 128
        qf, rq = big[0], R_big[0]
        kf, rk = big[1], R_big[1]
        sgb, rsg = big[2], R_big[2]
        tmp, rtmp = big[3], R_big[3]
        tmp2, rtmp2 = big[4], R_big[4]
        b5, r5 = big[5][:].bitcast(BF16), R_big[5]
        b6, r6 = big[6][:].bitcast(BF16), R_big[6]
        b7, r7 = big[7][:].bitcast(BF16), R_big[7]
        b8, r8 = big[8][:].bitcast(BF16), R_big[8]
        b9, r9 = big[9][:].bitcast(BF16), R_big[9]
        b10, r10 = big[10][:].bitcast(BF16), R_big[10]
        qtf, qtb = b5[:, 0:2048], b5[:, 2048:4096]
        ktf, ktb = b6[:, 0:2048], b6[:, 2048:4096]
        vb, krb = b7[:, 0:2048], b7[:, 2048:4096]
        vt, kdf = b8[:, 0:2048], b8[:, 2048:4096]
        kdb, ybf = b9[:, 0:2048], b9[:, 2048:4096]
        ropeC, ropeS = flex[:, 0:2048], flex[:, 2048:4096]

        vf = big[7]
        proj(0, G, 5 * W + h * 128, lambda tt, pt, rp: op(
            "act", lambda e: e.activation(sgb[:, sl(tt)], pt[:], AF.Silu), reads=(rp,), pw=(rsg,)))
        for d in range(2):
            dh = d * 8 + h
            eq = C_EQF if d == 0 else C_EQB
            op("act", lambda e, d=d, dh=dh, eq=eq: e.activation(rtab[:, d, :], eq, AF.Exp, scale=lgt[:, dh:dh + 1]),
               reads=(R_par, R_lgt), writes=(R_rtab,) if d == 0 else (), pw=() if d == 0 else (R_rtab,))
            op("act", lambda e, d=d, dh=dh, eq=eq: e.activation(rtab[:, 2 + d, :], eq, AF.Exp,
                                                                scale=lgt[:, 16 + dh:17 + dh]),
               reads=(R_par, R_lgt), pw=(R_rtab,))
            ek = c_ekf if d == 0 else c_ekb
            op("act", lambda e, d=d, dh=dh, ek=ek: e.activation(smallf[:, 52 + d:53 + d], ek, AF.Exp,
                                                                scale=lgt[:, dh:dh + 1]),
               reads=(R_par, R_lgt), writes=(R_small,) if d == 0 else (), pw=() if d == 0 else (R_small,))
            op("act", lambda e, d=d, dh=dh: e.activation(smallf[:, 54 + d:55 + d], c_128, AF.Exp,
                                                         scale=lgt[:, dh:dh + 1]),
               reads=(R_par, R_lgt), pw=(R_small,))
        if G.rope:
            for src, rs_ in ((qf, rq), (kf, rk)):
                for tt in range(T // 512):
                    pt, rp = next_ps()
                    op("pe", lambda e, src=src: e.matmul(pt[:], C_PERM, src[:, sl(tt)], start=True, stop=True),
                       reads=(rs_, R_par), writes=(rp,))
                    op("dve", lambda e: e.tensor_tensor(tmp[:, sl(tt)], pt[:], ropeS[:, sl(tt)], ALU.mult),
                       reads=(rp, R_flex[0]), pw=(rtmp,))
                op("dve", lambda e, src=src: e.tensor_tensor(src[:, 0:T], src[:, 0:T], ropeC[:, 0:T], ALU.mult),
                   reads=(R_flex[0],), writes=(rs_,))
                op("dve", lambda e, src=src: e.tensor_tensor(src[:, 0:T], src[:, 0:T], tmp[:, 0:T], ALU.add),
                   reads=(rtmp,), writes=(rs_,))

        def chv(ap):
            return ap[:, 0:T].rearrange("p (c i) -> p c i", i=128)

        def rowt(i):
            return rtab[:, i:i + 1, :].broadcast_to([128, nch, 128])
        op("dve", lambda e: e.tensor_tensor(chv(qtf), chv(qf), rowt(0), ALU.mult), reads=(rq, R_rtab), pw=(r5,))
        op("dve", lambda e: e.tensor_tensor(chv(qtb), chv(qf), rowt(1), ALU.mult), reads=(rq, R_rtab), pw=(r5,))
        op("dve", lambda e: e.tensor_tensor(chv(ktf), chv(kf), rowt(2), ALU.mult), reads=(rk, R_rtab), pw=(r6,))
        op("dve", lambda e: e.tensor_tensor(chv(ktb), chv(kf), rowt(3), ALU.mult), reads=(rk, R_rtab), pw=(r6,))
        op("act", lambda e: e.activation(krb[:, 0:T], kf[:, 0:T], AF.Copy), reads=(rk,), pw=(r7,))
        for c0 in range(0, nch, 4):
            for which in range(2):
                pt, rp = next_ps()
                src_, rsrc = (vb, r7) if which == 0 else (krb, r7)
                grp("pe", [(lambda e, j=j, src_=src_: e.matmul(pt[:, j * 128:(j + 1) * 128],
                                                               src_[:, (c0 + j) * 128:(c0 + j + 1) * 128], identb[:],
                                                               start=True, stop=True))
                           for j in range(4)], reads=(rsrc, R_par), writes=(rp,))
                if which == 0:
                    op("act", lambda e: e.activation(vt[:, c0 * 128:(c0 + 4) * 128], pt[:], AF.Copy),
                       reads=(rp,), pw=(r8,))
                else:
                    op("act", lambda e: e.activation(kdf[:, c0 * 128:(c0 + 4) * 128], pt[:], AF.Copy,
                                                     scale=smallf[:, 52:53]), reads=(rp, R_small), pw=(r8,))
                    op("act", lambda e: e.activation(kdb[:, c0 * 128:(c0 + 4) * 128], pt[:], AF.Copy,
                                                     scale=smallf[:, 53:54]), reads=(rp, R_small), pw=(r9,))
        hook()
        stt_ = big[3]

        def sview(s_, d_, slot):
            o_ = ((s_ * 2 + d_) * 2 + slot) * 128
            return stt_[:, o_:o_ + 128]
        if G.nseq == 1:
            for d in range(2):
                dma("sp", S_sst2[d], sview(0, d, 0), st_ret[d, h], writes=(R_ss[0][d][0],))
        else:
            op("dve", lambda e: e.memset(stt_[:], 0.0), writes=(R_big[3],))
        cur = {}
        for step in range(cps):
            for s_ in range(G.nseq):
                for d in range(2):
                    cu = cur.get((s_, d), 0)
                    ci = step if d == 0 else cps - 1 - step
                    c = s_ * cps + ci
                    kd = kdf if d == 0 else kdb
                    op("act", lambda e, d=d, c=c, s_=s_, cu=cu: e.activation(sbf[:, d, c, :], sview(s_, d, cu), AF.Copy),
                       reads=(R_ss[s_][d][cu],), pw=(R_sbf,))
                    pt, rp = next_ps()
                    op("pe", lambda e, c=c, kd=kd: e.matmul(pt[:, 0:128], kd[:, c * 128:(c + 1) * 128],
                                                            vt[:, c * 128:(c + 1) * 128], start=True, stop=True),
                       reads=(r8, r9), writes=(rp,))
                    op("dve", lambda e, d=d, s_=s_, cu=cu: e.scalar_tensor_tensor(
                        sview(s_, d, 1 - cu), sview(s_, d, cu), smallf[:, 54 + d:55 + d], pt[:, 0:128],
                        ALU.mult, ALU.add), reads=(rp, R_small, R_ss[s_][d][cu]), writes=(R_ss[s_][d][1 - cu],))
                    cur[(s_, d)] = 1 - cu
        if G.nseq > 1:
            for s_ in range(G.nseq):
                for d in range(2):
                    cu = cur[(s_, d)]
                    dma("sp", S_ost[s_][d], oret[s_, d, h], sview(s_, d, cu), reads=(R_ss[s_][d][cu],), pw=(R_out[2],))
        at = b10
        for c0 in range(0, nch, 4):
            po, rpo = next_acc()
            for j in range(4):
                c = c0 + j
                cs_ = slice(c * 128, (c + 1) * 128)
                ats = []
                atr = []
                for d in range(2):
                    kt_, qt_ = (ktf, qtf) if d == 0 else (ktb, qtb)
                    pt, rp = next_ps()
                    op("pe", lambda e, kt_=kt_, qt_=qt_: e.matmul(pt[:, 0:128], kt_[:, cs_], qt_[:, cs_],
                                                                  start=True, stop=True),
                       reads=(r5, r6), writes=(rp,))
                    ai = (cnt["st"] % 8)
                    cnt["st"] += 1
                    a_ap = at[:, ai * 128:(ai + 1) * 128]
                    op("dve", lambda e, d=d, a_ap=a_ap: e.tensor_tensor(a_ap, pt[:, 0:128], C_MF if d == 0 else C_MB,
                                                                        ALU.mult),
                       reads=(rp, R_par), writes=(R_at[ai],))
                    ats.append(a_ap)
                    atr.append(R_at[ai])
                grp("pe", [
                    lambda e: e.matmul(po[:, j * 128:(j + 1) * 128], vt[:, cs_], ats[0], start=True, stop=False),
                    lambda e: e.matmul(po[:, j * 128:(j + 1) * 128], vt[:, cs_], ats[1], start=False, stop=False),
                    lambda e: e.matmul(po[:, j * 128:(j + 1) * 128], sbf[:, 0, c, :], qtf[:, cs_], start=False, stop=False),
                    lambda e: e.matmul(po[:, j * 128:(j + 1) * 128], sbf[:, 1, c, :], qtb[:, cs_], start=False, stop=True),
                ], reads=(r8, atr[0], atr[1], R_sbf, r5), writes=((rpo,) if j == 0 else ()), pw=(() if j == 0 else (rpo,)))
            tsl = slice(c0 * 128, (c0 + 4) * 128)
            oi = (c0 // 4) % 2
            o_t, ro = big[10][:, 512 + oi * 512:1024 + oi * 512], R_ot[oi]
            osq, rosq = big[10][:, 1536 + 0:2048], R_ot[2]
            op("act", lambda e: e.activation(o_t, po[:], AF.Copy), reads=(rpo,), writes=(ro,))
            osqb = osq.bitcast(BF16)[:, 0:512]
            op("act", lambda e: e.activation(osqb, po[:], AF.Square), reads=(rpo,), writes=(rosq,))
            pm, rpm = next_ps()
            op("pe", lambda e: e.matmul(pm[:], onesb[:], osqb, start=True, stop=True),
               reads=(rosq, R_par), writes=(rpm,))
            op("act", lambda e: e.activation(osq, pm[:], AF.Ln, bias=c_eps),
               reads=(rpm, R_par), writes=(rosq,))
            op("act", lambda e: e.activation(osq, osq, AF.Exp, scale=-0.5), writes=(rosq,))
            op("dve", lambda e: e.tensor_tensor(o_t, o_t, osq, ALU.mult),
               reads=(rosq,), writes=(ro,))
            op("dve", lambda e: e.scalar_tensor_tensor(ybf[:, tsl], sgb[:, tsl], pf("rng", h), o_t,
                                                       ALU.mult, ALU.mult),
               reads=(rsg, ro, R_par), pw=(r9,))
        store_y(G, 8 + h, ybf[:, 0:T], r9, 9)

    def fourier(G):
        T = G.T
        nt = T // 128
        ABt = [big[3 + i][:].bitcast(BF16) for i in range(8)]
        rAB = [R_big[3 + i] for i in range(8)]
        xcb, rxc = big[0][:].bitcast(BF16), R_big[0]

        def ab_ap(tq, g4):
            flat = (tq * 4 + g4) * 512
            return ABt[flat // 4096][:, flat % 4096: flat % 4096 + 512], rAB[flat // 4096]
        for g4 in range(4):
            for ch in range(2):
                proj(1, G, g4 * 256 + ch * 128, lambda tt, pt, rp, ch=ch: op(
                    "act", lambda e: e.activation(xcb[:, ch * 2048 + tt * 512: ch * 2048 + (tt + 1) * 512], pt[:], AF.Copy),
                    reads=(rp,), pw=(rxc,)))
            for tq in range(nt):
                pt, rp = next_ps()
                grp("pe", [(lambda e, ch=ch: e.matmul(pt[:], xcb[:, ch * 2048 + tq * 128: ch * 2048 + (tq + 1) * 128],
                                                      dftcc[:, ch, :], start=(ch == 0), stop=(ch == 1)))
                           for ch in range(2)], reads=(rxc, R_par), writes=(rp,))
                dst, rd = ab_ap(tq, g4)
                eng = "act" if tq % 2 == 0 else "dve"
                if eng == "act":
                    op("act", lambda e, dst=dst: e.activation(dst, pt[:], AF.Copy), reads=(rp,), pw=(rd,))
                else:
                    op("dve", lambda e, dst=dst: e.tensor_copy(dst, pt[:]), reads=(rp,), pw=(rd,))
        L = G.L
        cps = L // 128
        KW = min(256, L)
        sgc, rsgc = big[1], R_big[1]
        ybf, rybf = big[2][:].bitcast(BF16), R_big[2]
        if G.nseq == 1:
            blocks = [(kt * 256, 256, [(0, kt)]) for kt in range(8)]
        else:
            blocks = [(0, T, [(s, 0) for s in range(G.nseq)])]
        tabC = flex[:].bitcast(BF16)[:, 0:4096].rearrange("p (c k) -> p c k", k=256)
        tabS = flex[:].bitcast(BF16)[:, 4096:8192].rearrange("p (c k) -> p c k", k=256)
        for (tok0, ntok, subs) in blocks:
            if G.nseq == 1:
                kt = subs[0][1]
                dma("sp", S_flex[0], tabC, dftC_d[:, kt * 256:(kt + 1) * 256].rearrange("(c p) k -> p c k", p=128),
                    writes=(R_flex[0],))
                dma("sp", S_flex[1], tabS, dftS_d[:, kt * 256:(kt + 1) * 256].rearrange("(c p) k -> p c k", p=128),
                    pw=(R_flex[0],))
            for c8 in range(8):
                g4, ch = c8 // 2, c8 % 2
                slot, rs = load_w(w_in[1], W + c8 * 128)
                bw = min(512, ntok)
                for t5 in range(ntok // bw):
                    pt, rp = next_ps()
                    tsl = slice(tok0 + t5 * bw, tok0 + (t5 + 1) * bw)
                    grp("pe", [(lambda e, k=k: e.matmul(pt[:, 0:bw], slot[:, k, :], H[:, k, tsl],
                                                        start=(k == 0), stop=(k == 15))) for k in range(16)],
                        reads=(rs, R_H), writes=(rp,))
                    op("act", lambda e: e.activation(sgc[:, tsl], pt[:, 0:bw], AF.Silu), reads=(rp,), pw=(rsgc,))
                for (s, kt) in subs:
                    pt, rp = next_ps()
                    mms = []
                    rr = set()
                    for part in range(2):
                        for tc_ in range(cps):
                            tq = s * cps + tc_
                            abp, rab = ab_ap(tq, g4)
                            rr.add(rab)
                            lhs = abp[:, part * 256 + ch * 128: part * 256 + (ch + 1) * 128]
                            if G.nseq == 1:
                                rhs = (tabC if part == 0 else tabS)[:, tc_, :]
                            else:
                                rhs = dft256[:, part, tc_, :]
                            first = (part == 0 and tc_ == 0)
                            last = (part == 1 and tc_ == cps - 1)
                            mms.append(lambda e, lhs=lhs, rhs=rhs, first=first, last=last: e.matmul(
                                pt[:, 0:KW], lhs, rhs, start=first, stop=last))
                    grp("pe", mms, reads=tuple(rr) + (R_flex[0], R_par), writes=(rp,))
                    o0 = s * L + kt * KW
                    op("dve", lambda e, o0=o0: e.tensor_tensor(ybf[:, o0:o0 + KW], pt[:, 0:KW], sgc[:, o0:o0 + KW], ALU.mult),
                       reads=(rp, rsgc), pw=(rybf,))
                    dma("sp", S_bs[2], yT_d[c8, :, G.t0 + o0:G.t0 + o0 + KW], ybf[:, o0:o0 + KW], reads=(rybf,), pw=(R_yT,))

    def sconv(G, c8):
        T, L = G.T, G.L
        cd, rcd = big[0], R_big[0]
        z, rz = big[1], R_big[1]
        cz, rcz = big[2], R_big[2]
        sg, rsg = big[3], R_big[3]
        ybf, ry = big[4][:].bitcast(BF16), R_big[4]
        proj(1, G, 3 * W + c8 * 128, lambda tt, pt, rp: op(
            "act", lambda e: e.activation(cd[:, sl(tt)], pt[:], AF.Copy), reads=(rp,), pw=(rcd,)))
        proj(1, G, 4 * W + c8 * 128, lambda tt, pt, rp: op(
            "dve", lambda e: e.tensor_tensor(z[:, sl(tt)], pt[:], cd[:, sl(tt)], ALU.mult), reads=(rp, rcd), pw=(rz,)))
        op("dve", lambda e: e.tensor_scalar(cz[:, 0:T], z[:, 0:T], pf("sw1", c8), None, ALU.mult),
           reads=(rz, R_par), writes=(rcz,))
        op("dve", lambda e: e.scalar_tensor_tensor(seqv(cz, G, 1, L), seqv(z, G, 0, L - 1), pf("sw0", c8),
                                                   seqv(cz, G, 1, L), ALU.mult, ALU.add),
           reads=(rz, R_par), writes=(rcz,))
        op("dve", lambda e: e.scalar_tensor_tensor(seqv(cz, G, 0, L - 1), seqv(z, G, 1, L), pf("sw2", c8),
                                                   seqv(cz, G, 0, L - 1), ALU.mult, ALU.add),
           reads=(rz, R_par), writes=(rcz,))
        proj(1, G, 5 * W + c8 * 128, lambda tt, pt, rp: op(
            "act", lambda e: e.activation(sg[:, sl(tt)], pt[:], AF.Silu), reads=(rp,), pw=(rsg,)))
        proj(1, G, 2 * W + c8 * 128, lambda tt, pt, rp: op(
            "dve", lambda e: e.tensor_tensor(cz[:, sl(tt)], pt[:], cz[:, sl(tt)], ALU.mult), reads=(rp,), writes=(rcz,)))
        op("dve", lambda e: e.tensor_tensor(ybf[:, 0:T], cz[:, 0:T], sg[:, 0:T], ALU.mult), reads=(rcz, rsg), writes=(ry,))
        store_y(G, 8 + c8, ybf[:, 0:T], ry, 4)

    def phase_c(l, G, xsrc, rx, xdst, rxd, last):
        T = G.T
        gate, rg = big[8], R_big[8]
        gexp, rgx = big[9], R_big[9]
        op("dve", lambda e: e.tensor_copy(gexp[:].rearrange("p (k m) -> p k m", m=128),
                                          modAB[:, l, G.row, 2, :].unsqueeze(2).broadcast_to([128, 16, 128])),
           reads=(R_modab[l][1],), writes=(rgx,))
        for c4 in range(4):
            pt, rp = next_ps()
            grp("pe", [(lambda e, j=j: e.matmul(pt[:, j * 128:(j + 1) * 128],
                                                gexp[:, (c4 * 4 + j) * 128:(c4 * 4 + j + 1) * 128], C_ID,
                                                start=True, stop=True)) for j in range(4)],
                reads=(rgx, R_par), writes=(rp,))
            op("act", lambda e: e.activation(gate[:, sl(c4)], pt[:], AF.Copy), reads=(rp,),
               writes=((rg,) if c4 == 0 else ()), pw=(() if c4 == 0 else (rg,)))
        for g in range(16):
            dma("sp", S_H, H[:, g, 0:T], yT_d[g, :, G.t0:G.t0 + T], reads=(R_yT,),
                writes=((R_H,) if g == 0 else ()), pw=(() if g == 0 else (R_H,)))
        Wo = [big[i][:].bitcast(BF16) for i in range(8)]
        for g in range(16):
            dma("pool", S_wo[g // 2], Wo[g // 2][:, (g % 2) * 2048:(g % 2 + 1) * 2048], w_out[l, g * 128:(g + 1) * 128, :],
                writes=((R_big[g // 2],) if g % 2 == 0 else ()), pw=(() if g % 2 == 0 else (R_big[g // 2],)))
        if last:
            fg = flex[:, 0:2048]
            junk = flex[:, 2048:3072].bitcast(BF16)
            dma("sp", S_flex[0], fg.rearrange("p (o n) -> p o n", o=1), fng_d.partition_broadcast(128),
                writes=(R_flex[0],))
        for tt in range(T // 128):
            i = cnt["ld"] % 2
            cnt["ld"] += 1
            xt, rxt = big[9 + i], R_big[9 + i]
            r0 = G.t0 + tt * 128
            dma("sp", S_bl[9 + i], xt[:], xsrc[r0:r0 + 128, :], reads=(rx,), writes=(rxt,))
            for dt_ in range(4):
                pt, rp = next_ps()
                grp("pe", [(lambda e, g=g: e.matmul(pt[:], H[:, g, tt * 128:(tt + 1) * 128],
                                                    Wo[g // 2][:, (g % 2) * 2048 + dt_ * 512:(g % 2) * 2048 + (dt_ + 1) * 512],
                                                    start=(g == 0), stop=(g == 15))) for g in range(16)],
                    reads=(R_H,) + tuple(R_big[0:8]), writes=(rp,))
                op("dve", lambda e: e.tensor_tensor(pt[:], pt[:], gate[:, sl(dt_)], ALU.mult),
                   reads=(rg,), writes=(rp,))
                op("dve", lambda e: e.tensor_tensor(xt[:, sl(dt_)], xt[:, sl(dt_)], pt[:], ALU.add),
                   reads=(rp,), writes=(rxt,))
            if last:
                sq = 56 + (tt % 2) * 3
                op("dve", lambda e: e.memset(smallf[:, sq:sq + 1], 0.0), writes=(R_small,))
                op("act", lambda e: e.activation(junk, xt[:], AF.Square, accum_out=smallf[:, sq:sq + 1]),
                   reads=(rxt, R_small), writes=(R_flex[1],), pw=(R_small,))
                op("act", lambda e: e.activation(smallf[:, sq + 1:sq + 2], smallf[:, sq:sq + 1], AF.Sqrt, scale=1.0 / D,
                                                 bias=c_eps), reads=(R_small, R_par), pw=(R_small,))
                op("dve", lambda e: e.reciprocal(smallf[:, sq + 2:sq + 3], smallf[:, sq + 1:sq + 2]),
                   reads=(R_small,), pw=(R_small,))
                op("dve", lambda e: e.scalar_tensor_tensor(xt[:], xt[:], smallf[:, sq + 2:sq + 3], fg, ALU.mult, ALU.mult),
                   reads=(R_small, R_flex[0]), writes=(rxt,))
                dma("sp", S_bs[9 + i], y_out[r0:r0 + 128, :], xt[:], reads=(rxt,), pw=(R_out[0],))
            else:
                dma("sp", S_bs[9 + i], xdst[r0:r0 + 128, :], xt[:], reads=(rxt,), pw=(rxd,))

    def final_norm():
        fg, rfg = big[2], R_big[2]
        junk, rj = big[3], R_big[3]
        dma("sp", S_bl[2], fg[:].rearrange("p (o n) -> p o n", o=1), fng_d.partition_broadcast(128), writes=(rfg,))
        for tt in range(TT // 128):
            i = cnt["ld"] % 2
            cnt["ld"] += 1
            xt, rxt = big[i], R_big[i]
            dma("sp", S_bl[i], xt[:], x2_d[tt * 128:(tt + 1) * 128, :], reads=(R_x2,), writes=(rxt,))
            op("dve", lambda e: e.memset(smallf[:, 48:49], 0.0), writes=(R_small,))
            op("act", lambda e: e.activation(junk[:], xt[:], AF.Square, accum_out=smallf[:, 48:49]),
               reads=(rxt, R_small), writes=(rj,), pw=(R_small,))
            op("act", lambda e: e.activation(smallf[:, 49:50], smallf[:, 48:49], AF.Sqrt, scale=1.0 / D, bias=c_eps),
               reads=(R_small, R_par), pw=(R_small,))
            op("dve", lambda e: e.reciprocal(smallf[:, 50:51], smallf[:, 49:50]),
               reads=(R_small,), pw=(R_small,))
            op("dve", lambda e: e.scalar_tensor_tensor(xt[:], xt[:], smallf[:, 50:51], fg[:], ALU.mult, ALU.mult),
               reads=(R_small, rfg), writes=(rxt,))
            dma("sp", S_bs[i], y_out[tt * 128:(tt + 1) * 128, :], xt[:], reads=(rxt,), pw=(R_out[0],))

    if stop_after == "SETUP":
        op("act", lambda e: e.activation(big[0][:, 0:64], lgt[:], AF.Copy), reads=(R_lgt,), writes=(R_big[0],))
        dma("sp", S_bs[0], y_out[0:128, 0:64], big[0][:, 0:64], reads=(R_big[0],), pw=(R_out[0],))
        tk.final_wait("sp", R_out)
        return nc
    for p_ in ada_pieces(0, 0, list(range(32)), ("ab",)):
        p_()
    bg_l1 = ada_pieces(1, 2, list(range(48)), ("ab", "g"))
    N_BG_L1 = len(bg_l1)
    bg_l0 = ada_pieces(0, 2, list(range(32, 48)), ("g",), col0=128) + bg_l1
    if stop_after == "ADA":
        op("act", lambda e: e.activation(big[0][:, 0:128], modAB[:].rearrange("p a b c d -> p (a b c d)"), AF.Copy),
           reads=(R_mod,), writes=(R_big[0],))
        dma("sp", S_bs[0], y_out[0:128, 0:128], big[0][:, 0:128], reads=(R_big[0],), pw=(R_out[0],))
        tk.final_wait("sp", R_out)
        return nc
    dma("sp", S_flex[0], flex[:, 0:2048], ropeC_d, writes=(R_flex[0],))
    dma("sp", S_flex[0], flex[:, 2048:4096], ropeS_d, pw=(R_flex[0],))
    for G in (GS, GP):
        phase_a(0, G, x_in, Res())
        if stop_after == "A0":
            break
        load_lru_w()
        if G is GS:
            bg.extend(bg_l0)
        lru_front(G, 0)
        for h in range(8):
            if h < 7:
                lru_back(G, h, lambda: lru_front(G, h + 1))
            else:
                lru_back(G, h, lambda: None, lambda: ret_front(G, 0))
        for h in range(8):
            if h < 7:
                ret_back(G, h, lambda: ret_front(G, h + 1))
            else:
                ret_back(G, h, lambda: None)
        if G is GS:
            bg_step(max(0, len(bg) - N_BG_L1))
        else:
            bg_step(len(bg))
        if stop_after == "B0":
            continue
        phase_c(0, G, x_in, Res(), x1_d, R_x1, False)
    if stop_after is None or stop_after in ("B0", "C0"):
        pt, rp = next_ps()
        op("pe", lambda e: e.matmul(pt[0:64, 0:128], stl[:], C_ID, start=True, stop=True),
           reads=(R_stl, R_par), writes=(rp,))
        op("act", lambda e: e.activation(big[0][0:64, 0:128], pt[0:64, 0:128], AF.Copy),
           reads=(rp,), writes=(R_big[0],))
        dma("sp", S_bs[0], olru, big[0][0:64, 0:128], reads=(R_big[0],), pw=(R_out[1],))
    if stop_after is None:
        rot["base"], rot["n"] = 0, 8
        for G in (GS, GP):
            phase_a(1, G, x1_d, R_x1)
            fourier(G)
            for c8 in range(8):
                sconv(G, c8)
            phase_c(1, G, x1_d, R_x1, x2_d, R_x2, True)
    tk.final_wait("sp", R_out + [R_x1, R_x2, R_yT])
    return nc


_CACHE = {}


def _prep_inputs(inp, core):
    b = core
    f32 = np.float32
    x = np.concatenate([inp["x_sample"][b], inp["x_prompt"][4 * b:4 * b + 4].reshape(TP, D)], axis=0)
    cond = np.stack([inp["c"][b], inp["c_ctx"]], axis=0)
    cond_fm = cond.reshape(2, 16, 128).transpose(2, 1, 0).reshape(128, 32)
    h0 = inp["state_lru"][b, 0].reshape(2, 8, 128).transpose(2, 0, 1).reshape(128, 16)
    pcore = np.ascontiguousarray(np.concatenate([cond_fm, h0], axis=1), dtype=f32)
    return {"x": np.ascontiguousarray(x, dtype=f32), "pcore": pcore,
            "st_ret": np.ascontiguousarray(inp["state_ret"][b, 0], dtype=f32)}


def kernel(**inputs):
    inp = {k: np.asarray(v) for k, v in inputs.items()}
    pfm = _pack_params(inp)
    consts = _host_consts()
    nc = build(pfm.shape[1])
    shared = {"pfm": pfm, "ada_w": np.ascontiguousarray(inp["ada_w"]), "ada_b": np.ascontiguousarray(inp["ada_b"]),
              "w_in": np.ascontiguousarray(inp["w_in"]), "w_out": np.ascontiguousarray(inp["w_out"]),
              "lru_w_r": np.ascontiguousarray(inp["lru_w_r"][0]), "lru_w_i": np.ascontiguousarray(inp["lru_w_i"][0]),
              "final_norm_g": np.ascontiguousarray(inp["final_norm_g"].reshape(1, D))}
    shared.update(consts)
    in_maps = []
    for c in range(NCORES):
        m = dict(shared)
        m.update(_prep_inputs(inp, c))
        in_maps.append(m)
    res = run_bass_kernel_spmd(nc, in_maps, core_ids=list(range(NCORES)))
    ys = np.stack([r["y"][0:TS] for r in res.results], axis=0)
    yp = np.concatenate([r["y"][TS:].reshape(4, 256, D) for r in res.results], axis=0)
    slru = np.concatenate([r["o_lru"].reshape(4, 1, 2, W) for r in res.results], axis=0)
    sret = np.concatenate([r["o_ret"].reshape(4, 1, 2, 8, 128, 128) for r in res.results], axis=0)
    return (yp.astype(np.float32), ys.astype(np.float32), slru.astype(np.float32), sret.astype(np.float32))
```
